# Optimizing a Trainium2 kernel written in Bass

```python
import jax, jax.numpy as jnp
from jax import lax
import numpy as np

D_MODEL = 1024
BATCH = 8
SEQ = 8192
DEPTH = 2

HEAD_DIM = 64
A_GROUPS = ((128, 1), (512, 4), (2048, 16))
N_GROUPS = len(A_GROUPS)
A_WIDTH = D_MODEL // 2
A_HEADS = A_WIDTH // HEAD_DIM
B_WIDTH = D_MODEL // 2
SC_WIDTH = 3
POOL_SIZES = (2, 4, 8, 16)
C_WIDTH = D_MODEL // 2
C_GROUP = C_WIDTH // len(POOL_SIZES)
D_WIDTH = D_MODEL // 2
D_CONV = 31
GATE_EVEN = A_WIDTH + B_WIDTH
GATE_ODD = C_WIDTH + D_WIDTH
EVEN_IN = 3 * N_GROUPS * A_WIDTH + 3 * B_WIDTH + GATE_EVEN
ODD_IN = C_WIDTH + 2 * D_WIDTH + GATE_ODD
ROT_DIM = HEAD_DIM // 4
ROPE_THETA = 500000.0
QBLK = 128
EPS = 1e-6
NEG = -1e30
N_EVEN = (DEPTH + 1) // 2
N_ODD = DEPTH // 2

kernel_name = "hybrid_dilated_attn_shortconv_pool_conformer"


def rms_norm(t, w):
    tf = t.astype(jnp.float32)
    tf = tf * lax.rsqrt(jnp.mean(tf * tf, axis=-1, keepdims=True) + EPS)
    return (tf * w.astype(jnp.float32)).astype(t.dtype)


def layer_norm(t, w, b):
    tf = t.astype(jnp.float32)
    mu = jnp.mean(tf, axis=-1, keepdims=True)
    var = jnp.mean(jnp.square(tf - mu), axis=-1, keepdims=True)
    y = (tf - mu) * lax.rsqrt(var + EPS)
    return (y * w.astype(jnp.float32) + b.astype(jnp.float32)).astype(t.dtype)


def rope_tables(positions):
    half = ROT_DIM // 2
    inv_freq = ROPE_THETA ** (-jnp.arange(half, dtype=jnp.float32) / half)
    ang = positions.astype(jnp.float32)[..., None] * inv_freq
    return jnp.cos(ang)[:, :, None, None, :], jnp.sin(ang)[:, :, None, None, :]


def apply_rope(t, cos, sin):
    half = ROT_DIM // 2
    t1 = t[..., :half].astype(jnp.float32)
    t2 = t[..., half:ROT_DIM].astype(jnp.float32)
    r1 = (t1 * cos - t2 * sin).astype(t.dtype)
    r2 = (t2 * cos + t1 * sin).astype(t.dtype)
    return jnp.concatenate([r1, r2, t[..., ROT_DIM:]], axis=-1)


def causal_dwconv(t, w):
    K, C = w.shape
    return lax.conv_general_dilated(
        t, w[:, None, :].astype(t.dtype), window_strides=(1,), padding=[(K - 1, 0)],
        dimension_numbers=('NWC', 'WIO', 'NWC'), feature_group_count=C)


def dilated_attention(q, k, v, window, dilation):
    Bsz, S, H, Dh = q.shape
    steps = window // dilation
    span = dilation * QBLK
    L = -(-S // span) * span
    n = L // dilation
    nb = n // QBLK

    def to_streams(t):
        t = jnp.pad(t, ((0, 0), (0, L - S), (0, 0), (0, 0)))
        t = t.reshape(Bsz, n, dilation, H, Dh).transpose(0, 2, 3, 1, 4)
        return t.reshape(Bsz, dilation, H, nb, QBLK, Dh)

    def with_prev(t):
        prev = jnp.pad(t, ((0, 0), (0, 0), (0, 0), (1, 0), (0, 0), (0, 0)))[:, :, :, :-1]
        return jnp.concatenate([prev, t], axis=-2)

    qb = to_streams(q)
    kc = with_prev(to_streams(k))
    vc = with_prev(to_streams(v))

    s = jnp.einsum('brhnqc,brhnkc->brhnqk', qb, kc).astype(jnp.float32) * (Dh ** -0.5)
    qi = jnp.arange(QBLK)[:, None] + QBLK
    kj = jnp.arange(2 * QBLK)[None, :]
    dist = qi - kj
    band = (dist >= 0) & (dist <= steps)
    blk = jnp.arange(nb)[:, None, None]
    mask = band[None] & ((blk > 0) | (kj[None] >= QBLK))
    s = jnp.where(mask, s, NEG)
    m = jnp.max(s, axis=-1, keepdims=True)
    p = jnp.exp(s - m)
    den = jnp.sum(p, axis=-1, keepdims=True)
    o = jnp.einsum('brhnqk,brhnkc->brhnqc', (p / den).astype(v.dtype), vc)
    lse = (m + jnp.log(den))[..., 0]

    o = o.reshape(Bsz, dilation, H, n, Dh).transpose(0, 3, 1, 2, 4).reshape(Bsz, L, H, Dh)[:, :S]
    lse = lse.reshape(Bsz, dilation, H, n).transpose(0, 3, 1, 2).reshape(Bsz, L, H)[:, :S]
    return o, lse


def causal_pool_minus_self(u):
    S = u.shape[1]
    cs = jnp.cumsum(u.astype(jnp.float32), axis=1)
    t = jnp.arange(S)
    outs = []
    for g, p in enumerate(POOL_SIZES):
        c = cs[:, :, g]
        prev = jnp.pad(c, ((0, 0), (p, 0), (0, 0)))[:, :S]
        cnt = jnp.minimum(t + 1, p).astype(jnp.float32)[None, :, None]
        outs.append((c - prev) / cnt - u[:, :, g].astype(jnp.float32))
    return jnp.stack(outs, axis=2).astype(u.dtype)


def even_layer(x, cos, sin, norm_w, w_in, q_norm_w, k_norm_w, conv_w, w_out):
    Bsz, S, _ = x.shape
    h = rms_norm(x, norm_w)
    proj = h @ w_in
    nA = N_GROUPS * A_WIDTH
    cuts = np.cumsum([nA, nA, nA, B_WIDTH, B_WIDTH, B_WIDTH]).tolist()
    q, k, v, bg, cg, hb, z = jnp.split(proj, cuts, axis=-1)
    shp = (Bsz, S, N_GROUPS, A_HEADS, HEAD_DIM)
    q = apply_rope(rms_norm(q.reshape(shp), q_norm_w), cos, sin)
    k = apply_rope(rms_norm(k.reshape(shp), k_norm_w), cos, sin)
    v = v.reshape(shp)
    outs, lses = [], []
    for g, (win, dil) in enumerate(A_GROUPS):
        o, l = dilated_attention(q[:, :, g], k[:, :, g], v[:, :, g], win, dil)
        outs.append(o)
        lses.append(l)
    wts = jax.nn.softmax(jnp.stack(lses, axis=0), axis=0)
    o_a = jnp.sum(wts[..., None] * jnp.stack(outs, axis=0).astype(jnp.float32), axis=0)
    o_a = o_a.astype(x.dtype).reshape(Bsz, S, A_WIDTH)
    y_b = bg * causal_dwconv(cg * hb, conv_w)
    u = jnp.concatenate([o_a, y_b], axis=-1) * jax.nn.silu(z)
    return x + u @ w_out


def odd_layer(x, norm_w, w_in, pool_w, pool_scale, dconv_w, dconv_b, ln_w, ln_b, w_out):
    Bsz, S, _ = x.shape
    h = rms_norm(x, norm_w)
    proj = h @ w_in
    cuts = np.cumsum([C_WIDTH, D_WIDTH, D_WIDTH]).tolist()
    uc, da, dg, z = jnp.split(proj, cuts, axis=-1)
    pooled = causal_pool_minus_self(uc.reshape(Bsz, S, len(POOL_SIZES), C_GROUP))
    y_c = jnp.einsum('bsgc,gcd->bsgd', pooled, pool_w).reshape(Bsz, S, C_WIDTH) * pool_scale
    gl = da * jax.nn.sigmoid(dg)
    c = causal_dwconv(gl, dconv_w) + dconv_b
    y_d = jax.nn.silu(layer_norm(c, ln_w, ln_b))
    u = jnp.concatenate([y_c, y_d], axis=-1) * jax.nn.silu(z)
    return x + u @ w_out


def setup_inputs(seed: int = 0) -> dict:
    key = jax.random.key(seed)
    ks = jax.random.split(key, 20)
    f32 = jnp.float32
    nrm = lambda k, shape, scale: jax.random.normal(k, shape, f32) * scale
    x = jax.random.normal(ks[0], (BATCH, SEQ, D_MODEL), f32)
    offset = jax.random.randint(ks[1], (BATCH, 1), 0, 4096, dtype=jnp.int32)
    positions = offset + jnp.arange(SEQ, dtype=jnp.int32)[None, :]
    return {
        "x": x,
        "positions": positions,
        "e_norm_w": 1.0 + nrm(ks[2], (N_EVEN, D_MODEL), 0.02),
        "e_w_in": nrm(ks[3], (N_EVEN, D_MODEL, EVEN_IN), D_MODEL ** -0.5),
        "e_q_norm_w": 1.0 + nrm(ks[4], (N_EVEN, HEAD_DIM), 0.02),
        "e_k_norm_w": 1.0 + nrm(ks[5], (N_EVEN, HEAD_DIM), 0.02),
        "e_conv_w": nrm(ks[6], (N_EVEN, SC_WIDTH, B_WIDTH), SC_WIDTH ** -0.5),
        "e_w_out": nrm(ks[7], (N_EVEN, GATE_EVEN, D_MODEL), GATE_EVEN ** -0.5),
        "o_norm_w": 1.0 + nrm(ks[8], (N_ODD, D_MODEL), 0.02),
        "o_w_in": nrm(ks[9], (N_ODD, D_MODEL, ODD_IN), D_MODEL ** -0.5),
        "o_pool_w": nrm(ks[10], (N_ODD, len(POOL_SIZES), C_GROUP, C_GROUP), C_GROUP ** -0.5),
        "o_pool_scale": 1.0 + nrm(ks[11], (N_ODD, C_WIDTH), 0.02),
        "o_dconv_w": nrm(ks[12], (N_ODD, D_CONV, D_WIDTH), D_CONV ** -0.5),
        "o_dconv_b": nrm(ks[13], (N_ODD, D_WIDTH), 0.01),
        "o_ln_w": 1.0 + nrm(ks[14], (N_ODD, D_WIDTH), 0.02),
        "o_ln_b": nrm(ks[15], (N_ODD, D_WIDTH), 0.01),
        "o_w_out": nrm(ks[16], (N_ODD, GATE_ODD, D_MODEL), GATE_ODD ** -0.5),
    }


def reference(x, positions, e_norm_w, e_w_in, e_q_norm_w, e_k_norm_w, e_conv_w, e_w_out,
              o_norm_w, o_w_in, o_pool_w, o_pool_scale, o_dconv_w, o_dconv_b, o_ln_w, o_ln_b,
              o_w_out):
    cos, sin = rope_tables(positions)
    for i in range(DEPTH):
        j = i // 2
        if i % 2 == 0:
            x = even_layer(x, cos, sin, e_norm_w[j], e_w_in[j], e_q_norm_w[j], e_k_norm_w[j],
                           e_conv_w[j], e_w_out[j])
        else:
            x = odd_layer(x, o_norm_w[j], o_w_in[j], o_pool_w[j], o_pool_scale[j], o_dconv_w[j],
                          o_dconv_b[j], o_ln_w[j], o_ln_b[j], o_w_out[j])
    return x
```

```python
import math
from contextlib import ExitStack
import numpy as np
import ml_dtypes
import concourse.bass as bass
import concourse.mybir as mybir
from concourse.bass_utils import run_bass_kernel_spmd

F32 = mybir.dt.float32
BF = mybir.dt.bfloat16
I32 = mybir.dt.int32
AF = mybir.ActivationFunctionType
ALU = mybir.AluOpType

S = 8192
D = 1024
EPS = 1e-6
GROUPS = ((128, 1), (512, 4), (2048, 16))
POOLS = (2, 4, 8, 16)
NCONV = 31
MAGIC = 12582912.0
TWO_PI = 2.0 * math.pi


def _cw_consts():
    c1 = np.float32(6.28125)
    rem = np.float64(TWO_PI) - np.float64(c1)
    c2 = np.float32(rem)
    c2 = np.frombuffer(np.uint32(np.frombuffer(np.float32(c2).tobytes(), np.uint32)[0] & 0xFFFFF000).tobytes(), np.float32)[0]
    c3 = np.float32(rem - np.float64(c2))
    return float(c1), float(c2), float(c3)


class Buf:
    __slots__ = ("name", "w", "r", "mw")

    def __init__(self, name):
        self.name = name
        self.w = {}
        self.r = {}
        self.mw = set()


class Prog:
    ENGS = ("pe", "act", "dve", "pool", "sp")

    def __init__(self, nc):
        self.nc = nc
        self.ops = []
        self.dma_cnt = {}
        self.dma_last = {}
        self.bufs = []
        self.base_w = {}

    def buf(self, name):
        b = Buf(name)
        b.w = dict(self.base_w)
        self.bufs.append(b)
        return b

    def barrier(self):
        allb = list(self.bufs)
        for k, i in self.dma_last.items():
            pass
        idx = self.add("sp", lambda e: e.nop(), reads=(), writes=allb)
        self.ops[idx][2].update(self.dma_last.values())
        self.ops[idx][2].discard(idx)
        self.base_w = {("e", "sp"): idx}
        return idx

    def add(self, eng, fn, reads=(), writes=(), dma=None, merge=False):
        idx = len(self.ops)
        deps = set()
        for b in reads:
            deps.update(b.w.values())
        for b in writes:
            if merge:
                deps.update(v for v in b.w.values() if v not in b.mw)
            else:
                deps.update(b.w.values())
            deps.update(b.r.values())
        val = None
        if dma is not None:
            if dma in self.dma_last:
                deps.add(self.dma_last[dma])
            self.dma_last[dma] = idx
            self.dma_cnt[dma] = self.dma_cnt.get(dma, 0) + 1
            val = 16 * self.dma_cnt[dma]
        self.ops.append((eng, fn, deps, dma, val))
        key = ("d", dma) if dma is not None else ("e", eng)
        for b in writes:
            if merge:
                b.w[key] = idx
                b.mw.add(idx)
            else:
                b.w = {key: idx}
                b.r = {}
                b.mw = set()
        for b in reads:
            if b not in writes:
                b.r[key] = idx
        return idx

    def emit(self, final_dmas=()):
        nc = self.nc
        ops = self.ops

        def skip(p, o):
            return p[3] is None and o[3] is None and p[0] == "pe" and o[0] == "pe"

        need = set()
        for o in ops:
            for d in o[2]:
                if not skip(ops[d], o):
                    need.add(d)
        esem = {e: nc.alloc_semaphore(name="sem_" + e) for e in self.ENGS}
        dsem = {k: nc.alloc_semaphore(name="dsem_" + str(k)) for k in self.dma_cnt}
        val = {}
        cnt = {e: 0 for e in self.ENGS}
        for i, o in enumerate(ops):
            if o[3] is not None:
                val[i] = ("d" + str(o[3]), dsem[o[3]], o[4], 16)
            elif i in need:
                cnt[o[0]] += 1
                val[i] = ("e" + o[0], esem[o[0]], cnt[o[0]], 1)
        finals = [(("d" + str(k)), dsem[k], 16 * self.dma_cnt[k]) for k in final_dmas]

        def run(name, e):
            seen = {}
            for i, o in enumerate(ops):
                if o[0] != name:
                    continue
                for d in sorted(o[2]):
                    if skip(ops[d], o):
                        continue
                    sk, sem, v, _ = val[d]
                    if seen.get(sk, 0) < v:
                        e.wait_ge(sem, v)
                        seen[sk] = v
                ins = o[1](e)
                if i in val:
                    ins.then_inc(val[i][1], val[i][3])
            if name == "sp":
                for sk, sem, v in finals:
                    if seen.get(sk, 0) < v:
                        e.wait_ge(sem, v)

        with nc.Block() as block:
            @block.tensor
            def _(e):
                run("pe", e)

            @block.scalar
            def _(e):
                run("act", e)

            @block.vector
            def _(e):
                run("dve", e)

            @block.gpsimd
            def _(e):
                run("pool", e)

            @block.sync
            def _(e):
                run("sp", e)


def build(phases="ABCD", dbg=False):
    nc = bass.Bass("TRN2", target_bir_lowering=False)
    P = Prog(nc)

    def din(name, shape, dt):
        return nc.dram_tensor(name, list(shape), dt, kind="ExternalInput")

    x_t = din("x", [S, D], F32)
    pos_t = din("pos", [1, S], I32)
    w_in0 = din("w_in0", [D, 7168], F32)
    nw0 = din("nw0", [1, D], F32)
    qkw = din("qkw", [128, 2], F32)
    invf = din("invf", [128, 1], F32)
    ssign = din("ssign", [128, 1], F32)
    ident_d = din("ident", [128, 128], BF)
    onesblk_d = din("onesblk", [128, 128], BF)
    rperm_d = din("rperm", [128, 128], BF)
    mask_d = din("mask", [128, 2, 512], BF)
    convw0 = din("convw0", [128, 4, 3], F32)
    w_out0 = din("w_out0", [D, D], F32)
    nw1 = din("nw1", [1, D], F32)
    w_in1 = din("w_in1", [D, 2560], F32)
    w_out1 = din("w_out1", [D, D], F32)
    poolw = din("poolw", [128, 4, 128], F32)
    pscale = din("pscale", [128, 4], F32)
    dconvw = din("dconvw", [128, 4, NCONV], F32)
    dconvb = din("dconvb", [128, 4], F32)
    lnw = din("lnw", [128, 4], F32)
    lnb = din("lnb", [128, 4], F32)
    cinv = din("cinv", [128, 4, 16], F32)
    onesall_d = din("onesall", [128, 128], BF)

    out_t = nc.dram_tensor("out", [S, D], F32, kind="ExternalOutput")
    kind_s = "ExternalOutput" if dbg else "Internal"
    qT = nc.dram_tensor("qT", [3, 4, 128, S], BF, kind=kind_s)
    kT = nc.dram_tensor("kT", [3, 4, 128, S], BF, kind=kind_s)
    vx = nc.dram_tensor("vx", [S, 24, 65], BF, kind=kind_s)
    og = nc.dram_tensor("og", [3, S, 520], F32, kind=kind_s)
    x1 = nc.dram_tensor("x1", [S, D], F32, kind=kind_s)
    if dbg:
        dbgC = nc.dram_tensor("dbgC", [16, 128, 8, 512], BF, kind="ExternalOutput")
        dbgD = nc.dram_tensor("dbgD", [16, 128, 8, 512], BF, kind="ExternalOutput")

    cur = [ExitStack()]

    def sb(name, shape, dt):
        return cur[0].enter_context(nc.sbuf_tensor("s_" + name, list(shape), dt))

    psum = [nc.alloc_psum_tensor("ps%d" % i, [128, 512], F32) for i in range(8)]
    psb = [P.buf("ps%d" % i) for i in range(8)]

    ident = sb("ident", [128, 128], BF)
    onesblk = sb("onesblk", [128, 128], BF)
    rperm = sb("rperm", [128, 128], BF)
    onesall = sb("onesall", [128, 128], BF)
    c_qkw = sb("c_qkw", [128, 2], F32)
    c_invf = sb("c_invf", [128, 1], F32)
    c_ssign = sb("c_ssign", [128, 1], F32)
    c_nw0 = sb("c_nw0", [128, D], F32)
    c_nw1 = c_nw0
    b_const = P.buf("const")
    c_eps = sb("c_eps", [128, 1], F32)
    P.add("pool", lambda e: e.memset(c_eps[:], EPS), writes=[b_const])
    for i, (dst, src) in enumerate([(ident, ident_d), (onesblk, onesblk_d), (rperm, rperm_d), (onesall, onesall_d),
                                    (c_qkw, qkw), (c_invf, invf), (c_ssign, ssign)]):
        P.add("sp", (lambda e, d=dst, s=src: e.dma_start(out=d[:], in_=s.ap())), writes=[b_const], dma="const")
    b_nw = P.buf("nw")
    P.add("sp", lambda e: e.dma_start(out=c_nw0[:], in_=bass.AP(nw0, 0, [[0, 128], [1, D]])), writes=[b_nw], dma="const")

    final_dmas = []

    xt = [sb("xt%d" % i, [128, D], F32) for i in range(4)]
    xt_b = [P.buf("xt%d" % i) for i in range(4)]
    hb = [sb("hb%d" % i, [128, D], BF) for i in range(4)]
    hb_b = [P.buf("hb%d" % i) for i in range(4)]
    st = [sb("st%d" % i, [128, 4], F32) for i in range(4)]
    st_b = [P.buf("st%d" % i) for i in range(4)]
    cnt_prep = [0]

    def xprep_a(src_ap, xi, nwt):
        hi = xi
        P.add("sp", lambda e: e.dma_start(out=xt[xi][:], in_=src_ap), writes=[xt_b[xi]], dma="xt%d" % xi)
        P.add("act", lambda e: e.activation(out=hb[hi][:], in_=xt[xi][:], func=AF.Square, accum_out=st[hi][:, 0:1]),
              reads=[xt_b[xi]], writes=[hb_b[hi], st_b[hi]])
        P.add("act", lambda e: e.activation(out=st[hi][:, 1:2], in_=st[hi][:, 0:1], func=AF.Ln, scale=1.0 / D, bias=c_eps[:, 0:1]),
              reads=[st_b[hi], b_const], writes=[st_b[hi]])
        P.add("act", lambda e: e.activation(out=st[hi][:, 2:3], in_=st[hi][:, 1:2], func=AF.Exp, scale=-0.5), reads=[st_b[hi]],
              writes=[st_b[hi]])
        P.add("dve", lambda e: e.scalar_tensor_tensor(out=hb[hi][:], in0=xt[xi][:], scalar=st[hi][:, 2:3], in1=nwt[:], op0=ALU.mult,
                                                      op1=ALU.mult), reads=[xt_b[xi], st_b[hi], b_nw], writes=[hb_b[hi]])

    def xprep_b(xi, hT, hT_b, col0, pbank):
        hi = xi
        i = cnt_prep[0]
        cnt_prep[0] += 1
        pv = psum[pbank][:].bitcast(BF)
        for k in range(8):
            P.add("pe", lambda e, k=k: e.transpose(out=pv[:, k * 128:(k + 1) * 128], in_=hb[hi][:, k * 128:(k + 1) * 128], identity=ident[:]),
                  reads=[hb_b[hi], b_const], writes=[psb[pbank]])
        src = pv.rearrange("p (k t) -> p k t", k=8)
        if i % 2 == 0:
            P.add("act", lambda e: e.copy(out=hT[:, :, col0:col0 + 128], in_=src), reads=[psb[pbank]], writes=[hT_b])
        else:
            P.add("dve", lambda e: e.tensor_copy(out=hT[:, :, col0:col0 + 128], in_=src), reads=[psb[pbank]], writes=[hT_b])

    def xprep(src_ap, xi, hT, hT_b, col0, pbank, nwt):
        xprep_a(src_ap, xi, nwt)
        xprep_b(xi, hT, hT_b, col0, pbank)

    lw_cnt = [0]

    def load_weight(dst, dst_b, w_dram, col0, ncols, *_unused):
        for c0 in range(0, ncols, 1280):
            cw = min(1280, ncols - c0)
            src = w_dram.ap()[:, col0 + c0:col0 + c0 + cw].rearrange("(k p) f -> p k f", p=128)
            key = "lw%d" % (lw_cnt[0] % 6)
            lw_cnt[0] += 1
            P.add("pool", lambda e, c0=c0, cw=cw, src=src: e.dma_start(out=dst[:, :, c0:c0 + cw], in_=src), writes=[dst_b], dma=key,
                  merge=True)

    wst = wst_b = None

    if "A" in phases:
        glob_stack = cur[0]
        cur[0] = ExitStack()
        wv = sb("wv", [128, 8, 1536], BF)
        wv_b = P.buf("wv")
        load_weight(wv, wv_b, w_in0, 3072, 1536)

        hT = sb("hT", [128, 8, 2048], BF)
        hT_b = P.buf("hT")
        tabC = sb("tabC", [128, 2048], F32)
        tabS = sb("tabS", [128, 2048], F32)
        tab_b = P.buf("tab")
        tmpA = sb("tmpA", [128, 2048], F32)
        tmpB = sb("tmpB", [128, 2048], F32)
        posi = sb("posi", [128, 2048], I32)
        tmp_b = P.buf("tmp")
        wch = [sb("wch%d" % i, [128, 8, 128], BF) for i in range(3)]
        wch_b = [P.buf("wch%d" % i) for i in range(3)]
        qw = [sb("qw%d" % i, [128, 512], BF) for i in range(2)]
        sq = [sb("sq%d" % i, [128, 512], BF) for i in range(2)]
        sd = [sb("sd%d" % i, [128, 512], F32) for i in range(2)]
        t1 = [sb("t1_%d" % i, [128, 512], F32) for i in range(2)]
        t2 = [sb("t2_%d" % i, [128, 512], F32) for i in range(2)]
        qw_b = [P.buf("qw%d" % i) for i in range(2)]
        sq_b = [P.buf("sq%d" % i) for i in range(2)]
        sd_b = [P.buf("sd%d" % i) for i in range(2)]
        t1_b = [P.buf("t1_%d" % i) for i in range(2)]
        t2_b = [P.buf("t2_%d" % i) for i in range(2)]
        qst = [sb("qst%d" % i, [128, 2048], BF) for i in range(2)]
        qst_b = [P.buf("qst%d" % i) for i in range(2)]
        vst = [sb("vst%d" % i, [128, 24, 65], BF) for i in range(2)]
        vst_b = [P.buf("vst%d" % i) for i in range(2)]
        for i in range(2):
            P.add("pool", lambda e, i=i: e.memset(vst[i][:, :, 64:65], 1.0), writes=[vst_b[i]])
        c1, c2, c3 = _cw_consts()
        PI_LO = 3.1415925

        wch_it = 0
        unit = 0
        vcount = 0
        for s in range(4):
            t0 = s * 2048
            for j in range(4):
                xprep_a(x_t.ap()[t0 + j * 128:t0 + (j + 1) * 128, :], j % 4, c_nw0)
            for j in range(16):
                xprep_b(j % 4, hT, hT_b, j * 128, 6 + (j % 2))
                if j + 4 < 16:
                    xprep_a(x_t.ap()[t0 + (j + 4) * 128:t0 + (j + 5) * 128, :], (j + 4) % 4, c_nw0)

            P.add("sp", lambda e, t0=t0: e.dma_start(out=posi[:], in_=bass.AP(pos_t, t0, [[0, 128], [1, 2048]])), writes=[tmp_b],
                  dma="posi")
            P.add("dve", lambda e: e.tensor_copy(out=tmpA[:], in_=posi[:]), reads=[tmp_b], writes=[tmp_b])
            P.add("dve", lambda e: e.tensor_scalar(out=tmpA[:], in0=tmpA[:], scalar1=c_invf[:, 0:1], scalar2=None, op0=ALU.mult),
                  reads=[tmp_b, b_const], writes=[tmp_b])
            P.add("dve", lambda e: e.tensor_scalar(out=tmpB[:], in0=tmpA[:], scalar1=1.0 / TWO_PI, scalar2=MAGIC, op0=ALU.mult, op1=ALU.add),
                  reads=[tmp_b], writes=[tmp_b])
            P.add("dve", lambda e: e.tensor_scalar(out=tmpB[:], in0=tmpB[:], scalar1=-MAGIC, scalar2=None, op0=ALU.add),
                  reads=[tmp_b], writes=[tmp_b])
            for cc_ in (c1, c2, c3):
                P.add("dve", lambda e, cc_=cc_: e.scalar_tensor_tensor(out=tmpA[:], in0=tmpB[:], scalar=-cc_, in1=tmpA[:], op0=ALU.mult,
                                                                     op1=ALU.add), reads=[tmp_b], writes=[tmp_b])
            P.add("dve", lambda e: e.tensor_scalar(out=tmpA[:], in0=tmpA[:], scalar1=PI_LO, scalar2=-PI_LO, op0=ALU.min, op1=ALU.max),
                  reads=[tmp_b], writes=[tmp_b])
            P.add("act", lambda e: e.activation(out=tabS[:], in_=tmpA[:], func=AF.Sin), reads=[tmp_b], writes=[tab_b])
            P.add("dve", lambda e: e.tensor_scalar(out=tabS[:], in0=tabS[:], scalar1=c_ssign[:, 0:1], scalar2=None, op0=ALU.mult),
                  reads=[tab_b, b_const], writes=[tab_b])
            P.add("dve", lambda e: e.tensor_scalar(out=tmpA[:], in0=tmpA[:], scalar1=math.pi / 2, scalar2=None, op0=ALU.add),
                  reads=[tmp_b], writes=[tmp_b])
            P.add("dve", lambda e: e.tensor_scalar(out=tmpB[:], in0=tmpA[:], scalar1=math.pi, scalar2=-TWO_PI, op0=ALU.is_gt, op1=ALU.mult),
                  reads=[tmp_b], writes=[tmp_b])
            P.add("dve", lambda e: e.tensor_tensor(out=tmpA[:], in0=tmpA[:], in1=tmpB[:], op=ALU.add), reads=[tmp_b], writes=[tmp_b])
            P.add("dve", lambda e: e.tensor_scalar(out=tmpA[:], in0=tmpA[:], scalar1=PI_LO, scalar2=-PI_LO, op0=ALU.min, op1=ALU.max),
                  reads=[tmp_b], writes=[tmp_b])
            P.add("act", lambda e: e.activation(out=tabC[:], in_=tmpA[:], func=AF.Sin), reads=[tmp_b, tab_b], writes=[tab_b])

            for j in range(16):
                vb = vcount % 2
                vcount += 1
                for g in range(3):
                    bank = 7 if (j * 3 + g) % 2 == 0 else 6
                    for k in range(8):
                        P.add("pe", lambda e, k=k, j=j, g=g, bank=bank: e.matmul(psum[bank][:], lhsT=hT[:, k, j * 128:(j + 1) * 128],
                                                                               rhs=wv[:, k, g * 512:(g + 1) * 512], start=(k == 0), stop=(k == 7)),
                              reads=[hT_b, wv_b], writes=[psb[bank]])
                    P.add("act", lambda e, vb=vb, g=g, bank=bank: e.copy(out=vst[vb][:, g * 8:(g + 1) * 8, 0:64],
                                                                       in_=psum[bank][:].rearrange("p (h c) -> p h c", h=8)),
                          reads=[psb[bank]], writes=[vst_b[vb]])
                dst = vx.ap()[t0 + j * 128:t0 + (j + 1) * 128, :, :]
                P.add("pool", lambda e, vb=vb, dst=dst: e.dma_start(out=dst, in_=vst[vb][:]), reads=[vst_b[vb]], dma="vst%d" % vb)

            def wload(ci_g):
                ci = ci_g % 24
                isk = ci >= 12
                g = (ci % 12) // 4
                cc = ci % 4
                col0 = (1536 if isk else 0) + g * 512 + cc * 128
                wb = ci_g % 3
                src = w_in0.ap()[:, col0:col0 + 128].rearrange("(k p) f -> p k f", p=128)
                P.add("pool", lambda e: e.dma_start(out=wch[wb][:], in_=src), writes=[wch_b[wb]], dma="wch%d" % wb)

            def stage1(ci, qd, u):
                isk = ci >= 12
                wb = (s * 24 + ci) % 3
                b = u % 2
                acc = b
                cs = slice(qd * 512, (qd + 1) * 512)
                for k in range(8):
                    P.add("pe", lambda e, k=k: e.matmul(psum[acc][:], lhsT=wch[wb][:, k, :], rhs=hT[:, k, cs], start=(k == 0), stop=(k == 7)),
                          reads=[wch_b[wb], hT_b], writes=[psb[acc]])
                P.add("act", lambda e: e.activation(out=qw[b][:], in_=psum[acc][:], func=AF.Copy,
                                                    scale=c_qkw[:, (1 if isk else 0):(2 if isk else 1)]),
                      reads=[psb[acc], b_const], writes=[qw_b[b]])
                P.add("act", lambda e: e.activation(out=sq[b][:], in_=psum[acc][:], func=AF.Square), reads=[psb[acc]], writes=[sq_b[b]])

            def stage2(ci, qd, u):
                isk = ci >= 12
                g = (ci % 12) // 4
                cc = ci % 4
                d = GROUPS[g][1]
                b = u % 2
                ssb = 2 + b
                rqb = 4 + b
                sbi = ci % 2
                cs = slice(qd * 512, (qd + 1) * 512)
                P.add("pe", lambda e: e.matmul(psum[ssb][:], lhsT=onesblk[:], rhs=sq[b][:], start=True, stop=True),
                      reads=[sq_b[b], b_const], writes=[psb[ssb]])
                P.add("pe", lambda e: e.matmul(psum[rqb][:], lhsT=rperm[:], rhs=qw[b][:], start=True, stop=True),
                      reads=[qw_b[b], b_const], writes=[psb[rqb]])
                P.add("act", lambda e: e.activation(out=sd[b][:], in_=psum[ssb][:], func=AF.Ln, scale=1.0 / 64, bias=c_eps[:, 0:1]),
                      reads=[psb[ssb], b_const], writes=[sd_b[b]])
                P.add("act", lambda e: e.activation(out=sd[b][:], in_=sd[b][:], func=AF.Exp, scale=-0.5), reads=[sd_b[b]], writes=[sd_b[b]])
                P.add("pool", lambda e: e.tensor_tensor(out=t1[b][:], in0=qw[b][:], in1=tabC[:, cs], op=ALU.mult),
                      reads=[qw_b[b], tab_b], writes=[t1_b[b]])
                P.add("dve", lambda e: e.tensor_tensor(out=t2[b][:], in0=psum[rqb][:], in1=tabS[:, cs], op=ALU.mult),
                      reads=[psb[rqb], tab_b], writes=[t2_b[b]])
                P.add("dve", lambda e: e.tensor_tensor(out=t2[b][:], in0=t2[b][:], in1=t1[b][:], op=ALU.add),
                      reads=[t1_b[b], t2_b[b]], writes=[t2_b[b]])
                nq = 512 // d
                dstv = qst[sbi][:].rearrange("f (r n) -> f r n", r=d)[:, :, qd * nq:(qd + 1) * nq]
                P.add("dve", lambda e: e.tensor_tensor(out=dstv, in0=t2[b][:].rearrange("f (n r) -> f r n", r=d),
                                                       in1=sd[b][:].rearrange("f (n r) -> f r n", r=d), op=ALU.mult),
                      reads=[t2_b[b], sd_b[b]], writes=[qst_b[sbi]])
                if qd == 3:
                    L = 2048 // d
                    tgt = kT if isk else qT
                    dst = tgt.ap()[g, cc].rearrange("f (r n) -> f r n", r=d)[:, :, s * L:(s + 1) * L]
                    P.add("sp", lambda e: e.dma_start(out=dst, in_=qst[sbi][:].rearrange("f (r n) -> f r n", r=d)),
                          reads=[qst_b[sbi]], dma="qst%d" % sbi)

            if s == 0:
                wload(0)
                wload(1)
            prev = None
            for ci in range(24):
                if s * 24 + ci + 2 < 96:
                    wload(s * 24 + ci + 2)
                for qd in range(4):
                    stage1(ci, qd, unit)
                    if prev is not None:
                        stage2(*prev)
                    prev = (ci, qd, unit)
                    unit += 1
            stage2(*prev)
        final_dmas += ["vst0", "vst1", "qst0", "qst1"]
        P.barrier()
        cur[0].close()
        cur[0] = glob_stack

    def bc3(t_ap, n_inner):
        return bass.AP(t_ap.tensor, t_ap.offset, [list(t_ap.ap[0]), list(t_ap.ap[1]), [0, n_inner]])

    if "B" in phases:
        glob_stack = cur[0]
        cur[0] = ExitStack()
        maskt = sb("maskt", [128, 2, 512], BF)
        mask_b = P.buf("mask")
        P.add("sp", lambda e: e.dma_start(out=maskt[:], in_=mask_d.ap()), writes=[mask_b], dma="maskld")
        LMAX = 1024
        NJB = 3
        qTj = [sb("qTj%d" % i, [128, 4, LMAX], BF) for i in range(NJB)]
        kTa = [sb("kTa%d" % i, [128, 4, LMAX + 128], BF) for i in range(NJB)]
        kTb = [sb("kTb%d" % i, [128, 4, LMAX + 128], BF) for i in range(NJB)]
        kTh = (kTa, kTb)
        vj = [sb("vj%d" % i, [128, LMAX // 128 + 1, 520], BF) for i in range(NJB)]
        qj_b = [P.buf("qj%d" % i) for i in range(NJB)]
        kj_b = [P.buf("kj%d" % i) for i in range(NJB)]
        ka_b = [P.buf("ka%d" % i) for i in range(NJB)]
        kb_b = [P.buf("kb%d" % i) for i in range(NJB)]
        va_b = [P.buf("va%d" % i) for i in range(NJB)]
        vb_b = [P.buf("vb%d" % i) for i in range(NJB)]
        for i in range(NJB):
            P.add("pool", lambda e, i=i: e.memset(kTa[i][:], 0.0), writes=[ka_b[i]])
            P.add("pool", lambda e, i=i: e.memset(kTb[i][:], 0.0), writes=[kb_b[i]])
        kj_init = kj_b
        vj_b = [P.buf("vj%d" % i) for i in range(NJB)]
        pT = [sb("pT%d" % i, [128, 512], BF) for i in range(5)]
        pT_b = [P.buf("pT%d" % i) for i in range(5)]
        ost = [sb("ost%d" % i, [128, 8, 65], F32) for i in range(2)]
        ost_b = [P.buf("ost%d" % i) for i in range(2)]
        jobs = []
        for g, (w_, d) in enumerate(GROUPS):
            N = S // d
            L = min(LMAX, N)
            for r in range(d):
                for seg in range(N // L):
                    jobs.append((g, d, r, seg * L, L))

        def job_loads(ji):
            g, d, r, n0, L = jobs[ji]
            jb = ji % NJB
            nblk = L // 128
            qsrc = qT.ap()[g].rearrange("c f (r n) -> f c r n", r=d)[:, :, r, n0:n0 + L]
            P.add("sp", lambda e: e.dma_start(out=qTj[jb][:, :, 0:L], in_=qsrc), writes=[qj_b[jb]], dma="qj%d" % jb)
            kview = kT.ap()[g].rearrange("c f (r n) -> f c r n", r=d)
            if n0 == 0:
                P.add("pool", lambda e: e.memset(kTa[jb][0:64, :, 0:128], 0.0), writes=[ka_b[jb]])
                P.add("pool", lambda e: e.memset(kTb[jb][64:128, :, 0:128], 0.0), writes=[kb_b[jb]])
                P.add("pool", lambda e: e.memset(vj[jb][:, 0, :], 0.0), writes=[va_b[jb]])
                bb0 = 1
            else:
                bb0 = 0
            c0 = 128 * bb0
            for hh_, kt_, kb_ in ((0, kTa, ka_b), (1, kTb, kb_b)):
                ps_ = slice(hh_ * 64, (hh_ + 1) * 64)
                P.add("sp", lambda e, kt_=kt_, ps_=ps_: e.dma_start(out=kt_[jb][ps_, :, c0:128 + L],
                                                                  in_=kview[ps_, :, r, n0 - 128 + c0:n0 + L]),
                      writes=[kb_[jb]], dma="k%d_%d" % (hh_, jb))
            for pi_, b0_ in enumerate(range(bb0, nblk + 1, 5)):
                nb_ = min(5, nblk + 1 - b0_)
                off = (r + d * (n0 - 128 + 128 * b0_)) * 1560 + g * 520
                vsrc = bass.AP(vx, off, [[d * 1560, 128], [128 * d * 1560, nb_], [1, 520]])
                vb_ = va_b if pi_ == 0 else vb_b
                P.add("sp", lambda e, b0_=b0_, nb_=nb_, vsrc=vsrc: e.dma_start(out=vj[jb][:, b0_:b0_ + nb_, :], in_=vsrc),
                      writes=[vb_[jb]], dma="v%d_%d" % (pi_, jb))

        units = []
        for ji, (g, d, r, n0, L) in enumerate(jobs):
            for b in range(L // 128):
                for hp in range(4):
                    units.append((ji, b, hp))
        ucount = [0]
        ocount = [0]

        def stage1(u):
            ji, b, hp = units[u]
            g, d, r, n0, L = jobs[ji]
            jb = ji % NJB
            sbk = (0, 1, 2, 7)[u % 4]
            pp = u % 5
            for hh in range(2):
                for half in range(2):
                    P.add("pe", lambda e, hh=hh, half=half: e.matmul(
                        psum[sbk][:, (hh * 2 + half) * 128:(hh * 2 + half + 1) * 128],
                        lhsT=kTh[hh][jb][:, hp, (b + half) * 128:(b + half + 1) * 128],
                        rhs=qTj[jb][:, hp, b * 128:(b + 1) * 128], start=True, stop=True),
                        reads=[(ka_b, kb_b)[hh][jb], qj_b[jb]], writes=[psb[sbk]])
            P.add("act", lambda e: e.activation(out=pT[pp][:], in_=psum[sbk][:], func=AF.Exp, scale=0.125), reads=[psb[sbk]],
                  writes=[pT_b[pp]])
            mi = 1 if (n0 + 128 * b == 0) else 0
            P.add("dve", lambda e: e.tensor_tensor(out=pT[pp][:], in0=pT[pp][:], in1=maskt[:, mi, :], op=ALU.mult),
                  reads=[pT_b[pp], mask_b], writes=[pT_b[pp]])

        def stage2(u):
            ji, b, hp = units[u]
            g, d, r, n0, L = jobs[ji]
            jb = ji % NJB
            pp = u % 5
            oset = (u // 4) % 2
            banks = (3 + 2 * oset, 4 + 2 * oset)
            for hh in range(2):
                head = hp * 2 + hh
                ob = banks[head // 4]
                for half in range(2):
                    P.add("pe", lambda e, hh=hh, half=half, head=head, ob=ob: e.matmul(
                        psum[ob][:, (head % 4) * 65:(head % 4 + 1) * 65],
                        lhsT=pT[pp][:, (hh * 2 + half) * 128:(hh * 2 + half + 1) * 128],
                        rhs=vj[jb][:, b + half, head * 65:(head + 1) * 65], start=(half == 0), stop=(half == 1)),
                        reads=[pT_b[pp], va_b[jb], vb_b[jb]], writes=[psb[ob]])
            if hp == 3:
                evq.append((u, g, d, r, n0, b, banks))

        evq = []

        def evac(item):
            u, g, d, r, n0, b, banks = item
            oi = ocount[0] % 2
            ocount[0] += 1
            P.add("dve", lambda e: e.tensor_copy(out=ost[oi][:, 0:4, :], in_=psum[banks[0]][:, 0:260].rearrange("p (h c) -> p h c", h=4)),
                  reads=[psb[banks[0]]], writes=[ost_b[oi]])
            P.add("dve", lambda e: e.tensor_copy(out=ost[oi][:, 4:8, :], in_=psum[banks[1]][:, 0:260].rearrange("p (h c) -> p h c", h=4)),
                  reads=[psb[banks[1]]], writes=[ost_b[oi]])
            dst = bass.AP(og, (g * S + r + d * (n0 + 128 * b)) * 520, [[d * 520, 128], [1, 520]])
            P.add("pool", lambda e: e.dma_start(out=dst, in_=ost[oi][:].rearrange("p h c -> p (h c)")), reads=[ost_b[oi]],
                  dma="ost%d" % oi)

        LAG = 3
        job_loads(0)
        first_unit = {}
        for u, (ji, b, hp) in enumerate(units):
            if b == 0 and hp == 0:
                first_unit[ji] = u
        next_load = 1
        for u in range(len(units)):
            stage1(u)
            if u >= LAG:
                stage2(u - LAG)
            while evq and evq[0][0] + LAG + 2 <= u:
                evac(evq.pop(0))
            if next_load < len(jobs) and (next_load < NJB or u - LAG >= first_unit[next_load - NJB + 1] - 1):
                job_loads(next_load)
                next_load += 1
        for u in range(len(units) - LAG, len(units)):
            stage2(u)
        while evq:
            evac(evq.pop(0))
        final_dmas += ["ost0", "ost1"]
        P.barrier()
        cur[0].close()
        cur[0] = glob_stack

    def out_proj(uT, uT_b, wo, wo_b, xo, xo_b, res_t, dst_t, t0, tagkey, banks, cnt):
        it = 0
        ois = []
        for sub in range(4):
            ois.append(cnt[0] % len(xo))
            cnt[0] += 1

        def reload(sub):
            oi = ois[sub]
            rows = slice(t0 + sub * 128, t0 + (sub + 1) * 128)
            P.add("sp", lambda e: e.dma_start(out=xo[oi][:], in_=res_t.ap()[rows, :]), writes=[xo_b[oi]], dma=tagkey + "r" + str(oi))

        H = min(4, len(xo))
        for sub in range(H):
            reload(sub)
        for sub in range(4):
            oi = ois[sub]
            rows = slice(t0 + sub * 128, t0 + (sub + 1) * 128)
            for half in range(2):
                bank = banks[it % len(banks)]
                it += 1
                for cc in range(8):
                    P.add("pe", lambda e, cc=cc, sub=sub, half=half, bank=bank: e.matmul(
                        psum[bank][:], lhsT=uT[:, cc, sub * 128:(sub + 1) * 128], rhs=wo[:, cc, half * 512:(half + 1) * 512],
                        start=(cc == 0), stop=(cc == 7)), reads=[uT_b, wo_b], writes=[psb[bank]])
                P.add("dve", lambda e, half=half, bank=bank, oi=oi: e.tensor_tensor(
                    out=xo[oi][:, half * 512:(half + 1) * 512], in0=psum[bank][:], in1=xo[oi][:, half * 512:(half + 1) * 512], op=ALU.add),
                    reads=[psb[bank], xo_b[oi]], writes=[xo_b[oi]])
            P.add("pool", lambda e, oi=oi, rows=rows: e.dma_start(out=dst_t.ap()[rows, :], in_=xo[oi][:]), reads=[xo_b[oi]],
                  dma=tagkey + str(oi))
            if sub + H < 4:
                reload(sub + H)

    def in_proj(w, w_b, col0, hTt, hT_b, bank):
        for k in range(8):
            P.add("pe", lambda e, k=k: e.matmul(psum[bank][:], lhsT=w[:, k, col0:col0 + 128], rhs=hTt[:, k, :], start=(k == 0), stop=(k == 7)),
                  reads=[w_b, hT_b], writes=[psb[bank]])

    if "C" in phases:
        glob_stack = cur[0]
        cur[0] = ExitStack()
        wbz = sb("wbz", [128, 8, 2560], BF)
        wbz_b = P.buf("wbz")
        load_weight(wbz, wbz_b, w_in0, 4608, 2560)
        wo0 = sb("wo0", [128, 8, 1024], BF)
        wo0_b = P.buf("wo0")
        load_weight(wo0, wo0_b, w_out0, 0, 1024)
        c_cw = sb("c_cw", [128, 4, 3], F32)
        cc_b = P.buf("c_cw")
        P.add("sp", lambda e: e.dma_start(out=c_cw[:], in_=convw0.ap()), writes=[cc_b], dma="ccw")
        hTc2 = [sb("hTc%d" % i, [128, 8, 512], BF) for i in range(2)]
        hTc2_b = [P.buf("hTc%d" % i) for i in range(2)]
        gi = sb("gi", [128, 4, 514], F32)
        gi_b = [P.buf("gi%d" % i) for i in range(4)]
        P.add("pool", lambda e: e.memset(gi[:], 0.0), writes=gi_b)
        tmpc = [sb("tmpc%d" % i, [128, 512], F32) for i in range(2)]
        tmpc_b = [P.buf("tmpc%d" % i) for i in range(2)]
        ycv = [sb("ycv%d" % i, [128, 512], F32) for i in range(2)]
        ycv_b = [P.buf("ycv%d" % i) for i in range(2)]
        yb = sb("yb", [128, 4, 512], F32)
        yb_b = [P.buf("yb%d" % i) for i in range(4)]
        sz = sb("sz", [128, 8, 512], BF)
        sz_b = [P.buf("sz%d" % i) for i in range(8)]
        uT2 = [sb("uT%d" % i, [128, 8, 512], BF) for i in range(2)]
        uT2_b = [P.buf("uT%d" % i) for i in range(2)]
        ogt = [sb("ogt%d" % i, [128, 8, 65], F32) for i in range(12)]
        ogt_b = [P.buf("ogt%d" % i) for i in range(12)]
        rden = [sb("rden%d" % i, [128, 8], F32) for i in range(4)]
        rden_b = [P.buf("rden%d" % i) for i in range(4)]
        oab = [sb("oab%d" % i, [128, 512], BF) for i in range(4)]
        oab_b = [P.buf("oab%d" % i) for i in range(4)]
        xo = [sb("xo%d" % i, [128, 1024], F32) for i in range(4)]
        xo_b = [P.buf("xo%d" % i) for i in range(4)]
        acnt_ = [0]
        xocnt = [0]

        def C_xa(ti):
            t0 = ti * 512
            for sub in range(4):
                xprep_a(x_t.ap()[t0 + sub * 128:t0 + (sub + 1) * 128, :], sub, c_nw0)

        def C_ol(ti):
            t0 = ti * 512
            for sub in range(4):
                tt = t0 + sub * 128
                for g in range(3):
                    oi_ = sub * 3 + g
                    P.add("sp", lambda e, g=g, oi_=oi_, tt=tt: e.dma_start(out=ogt[oi_][:].rearrange("p h c -> p (h c)"),
                                                                         in_=og.ap()[g, tt:tt + 128, :]),
                          writes=[ogt_b[oi_]], dma="ogt%d" % oi_)

        def C_xb(ti):
            for sub in range(4):
                xprep_b(sub, hTc2[ti % 2], hTc2_b[ti % 2], sub * 128, 7 if sub % 2 == 0 else 2)

        def C_s1(ti):
            t0 = ti * 512
            hTc, hTc_b, uT, uT_b = hTc2[ti % 2], hTc2_b[ti % 2], uT2[ti % 2], uT2_b[ti % 2]
            for zc in range(8):
                bank = zc % 2
                in_proj(wbz, wbz_b, 1536 + zc * 128, hTc, hTc_b, bank)
                P.add("act", lambda e, zc=zc, bank=bank: e.activation(out=sz[:, zc, :], in_=psum[bank][:], func=AF.Silu),
                      reads=[psb[bank]], writes=[sz_b[zc]])
            for cc in range(4):
                tb = cc % 2
                bcg, bhb = (2, 3) if cc % 2 == 0 else (6, 7)
                in_proj(wbz, wbz_b, 512 + cc * 128, hTc, hTc_b, bcg)
                in_proj(wbz, wbz_b, 1024 + cc * 128, hTc, hTc_b, bhb)
                in_proj(wbz, wbz_b, cc * 128, hTc, hTc_b, 4 + tb)
                P.add("act", lambda e, tb=tb, bcg=bcg: e.copy(out=tmpc[tb][:], in_=psum[bcg][:]), reads=[psb[bcg]], writes=[tmpc_b[tb]])
                P.add("dve", lambda e, tb=tb, cc=cc, bhb=bhb: e.tensor_tensor(out=gi[:, cc, 2:514], in0=tmpc[tb][:], in1=psum[bhb][:], op=ALU.mult),
                      reads=[tmpc_b[tb], psb[bhb]], writes=[gi_b[cc]])
                P.add("dve", lambda e, tb=tb, cc=cc: e.tensor_scalar(out=ycv[tb][:], in0=gi[:, cc, 0:512], scalar1=c_cw[:, cc, 0:1], scalar2=None,
                                                                   op0=ALU.mult), reads=[gi_b[cc], cc_b], writes=[ycv_b[tb]])
                for kk in (1, 2):
                    P.add("dve", lambda e, tb=tb, cc=cc, kk=kk: e.scalar_tensor_tensor(out=ycv[tb][:], in0=gi[:, cc, kk:kk + 512],
                                                                                    scalar=c_cw[:, cc, kk:kk + 1], in1=ycv[tb][:],
                                                                                    op0=ALU.mult, op1=ALU.add),
                          reads=[gi_b[cc], cc_b, ycv_b[tb]], writes=[ycv_b[tb]])
                P.add("dve", lambda e, tb=tb, cc=cc: e.tensor_tensor(out=yb[:, cc, :], in0=ycv[tb][:], in1=psum[4 + tb][:], op=ALU.mult),
                      reads=[ycv_b[tb], psb[4 + tb]], writes=[yb_b[cc]])
                P.add("pool", lambda e, cc=cc: e.tensor_copy(out=gi[:, cc, 0:2], in_=gi[:, cc, 512:514]), reads=[gi_b[cc]], writes=[gi_b[cc]])
                P.add("pool", lambda e, cc=cc: e.tensor_tensor(out=uT[:, 4 + cc, :], in0=yb[:, cc, :], in1=sz[:, 4 + cc, :], op=ALU.mult),
                      reads=[yb_b[cc], sz_b[4 + cc]], writes=[uT_b])

        def C_att_pre(ti):
            for sub in range(4):
                ai = sub
                a0, a1, a2 = sub * 3, sub * 3 + 1, sub * 3 + 2
                P.add("pool", lambda e, a0=a0, a1=a1: e.tensor_tensor(out=ogt[a0][:], in0=ogt[a0][:], in1=ogt[a1][:], op=ALU.add),
                      reads=[ogt_b[a0], ogt_b[a1]], writes=[ogt_b[a0]])
                P.add("pool", lambda e, a0=a0, a2=a2: e.tensor_tensor(out=ogt[a0][:], in0=ogt[a0][:], in1=ogt[a2][:], op=ALU.add),
                      reads=[ogt_b[a0], ogt_b[a2]], writes=[ogt_b[a0]])
                P.add("dve", lambda e, a0=a0, ai=ai: e.reciprocal(out=rden[ai][:], in_=ogt[a0][:, :, 64]), reads=[ogt_b[a0]],
                      writes=[rden_b[ai]])
                P.add("dve", lambda e, a0=a0, ai=ai: e.tensor_tensor(out=oab[ai][:].rearrange("p (h c) -> p h c", h=8), in0=ogt[a0][:, :, 0:64],
                                                                   in1=bc3(rden[ai][:, :], 64), op=ALU.mult),
                      reads=[ogt_b[a0], rden_b[ai]], writes=[oab_b[ai]])

        def C_att(ti):
            uT, uT_b = uT2[ti % 2], uT2_b[ti % 2]
            for sub in range(4):
                ai = sub
                pbk = 6 if sub % 2 == 0 else 5
                pv = psum[pbk][:].bitcast(BF)
                for cc in range(4):
                    P.add("pe", lambda e, cc=cc, ai=ai, pv=pv: e.transpose(out=pv[:, cc * 128:(cc + 1) * 128],
                                                                         in_=oab[ai][:, cc * 128:(cc + 1) * 128], identity=ident[:]),
                          reads=[oab_b[ai], b_const], writes=[psb[pbk]])
                P.add("dve", lambda e, sub=sub, pv=pv: e.tensor_tensor(out=uT[:, 0:4, sub * 128:(sub + 1) * 128],
                                                                     in0=pv[:, 0:512].rearrange("p (c t) -> p c t", c=4),
                                                                     in1=sz[:, 0:4, sub * 128:(sub + 1) * 128], op=ALU.mult),
                      reads=[psb[pbk], sz_b[0], sz_b[1], sz_b[2], sz_b[3]], writes=[uT_b])

        def C_out(ti):
            out_proj(uT2[ti % 2], uT2_b[ti % 2], wo0, wo0_b, xo, xo_b, x_t, x1, ti * 512, "xoC", (0, 1), xocnt)

        C_xa(0)
        C_ol(0)
        C_xb(0)
        C_att_pre(0)
        C_s1(0)
        C_att(0)
        C_xa(1)
        C_ol(1)
        C_xb(1)
        for ti in range(16):
            if ti + 2 < 16:
                C_xa(ti + 2)
            if ti + 1 < 16:
                C_att_pre(ti + 1)
                C_s1(ti + 1)
                C_att(ti + 1)
            if ti + 2 < 16:
                C_ol(ti + 2)
                C_xb(ti + 2)
            C_out(ti)
        final_dmas += ["xoC0", "xoC1", "xoC2", "xoC3"]
        P.barrier()
        cur[0].close()
        cur[0] = glob_stack

    if "D" in phases:
        glob_stack = cur[0]
        cur[0] = ExitStack()
        P.add("sp", lambda e: e.dma_start(out=c_nw0[:], in_=bass.AP(nw1, 0, [[0, 128], [1, D]])), writes=[b_nw], dma="const")
        w1 = sb("w1", [128, 8, 2560], BF)
        w1_b = P.buf("w1")
        load_weight(w1, w1_b, w_in1, 0, 2560)
        wo1 = sb("wo1", [128, 8, 1024], BF)
        wo1_b = P.buf("wo1")
        load_weight(wo1, wo1_b, w_out1, 0, 1024)
        cD = P.buf("cD")
        c_pw32 = sb("c_pw32", [128, 4, 128], F32)
        c_ps = sb("c_ps", [128, 4], F32)
        c_dw = sb("c_dw", [128, 4, NCONV], F32)
        c_db = sb("c_db", [128, 4], F32)
        c_lw = sb("c_lw", [128, 4], F32)
        c_lb = sb("c_lb", [128, 4], F32)
        c_ci = sb("c_ci", [128, 4, 16], F32)
        for dst_, src_ in [(c_pw32, poolw), (c_ps, pscale), (c_dw, dconvw), (c_db, dconvb), (c_lw, lnw), (c_lb, lnb), (c_ci, cinv)]:
            P.add("sp", lambda e, d_=dst_, s_=src_: e.dma_start(out=d_[:], in_=s_.ap()), writes=[cD], dma="cD")
        pw = sb("pw", [128, 4, 128], BF)
        pw_b = P.buf("pw")
        P.add("dve", lambda e: e.tensor_copy(out=pw[:], in_=c_pw32[:]), reads=[cD], writes=[pw_b])
        diag = sb("diag", [128, 4, NCONV, 128], BF)
        diag_b = P.buf("diag")
        for cc in range(4):
            for k in range(NCONV):
                eng = "dve"
                P.add(eng, lambda e, cc=cc, k=k: e.tensor_scalar(out=diag[:, cc, k, :], in0=ident[:], scalar1=c_dw[:, cc, k:k + 1], scalar2=None,
                                                               op0=ALU.mult), reads=[cD, b_const], writes=[diag_b])
        hTd = sb("hTd", [128, 8, 512], BF)
        hTd_b = P.buf("hTd")
        szd = sb("szD", [128, 8, 512], BF)
        szd_b = [P.buf("szD%d" % i) for i in range(8)]
        uce = sb("uce", [128, 4, 528], F32)
        uce_b = [P.buf("uce%d" % i) for i in range(4)]
        P.add("pool", lambda e: e.memset(uce[:], 0.0), writes=uce_b)
        sA = sb("sA", [128, 528], F32)
        sB = sb("sB", [128, 528], F32)
        sAB_b = P.buf("sAB")
        pooled = [sb("pooled%d" % i, [128, 512], BF) for i in range(4)]
        pooled_b = [P.buf("pooled%d" % i) for i in range(4)]
        gle = sb("gle", [128, 4, 544], BF)
        gle_b = [P.buf("gle%d" % i) for i in range(4)]
        P.add("pool", lambda e: e.memset(gle[:], 0.0), writes=gle_b)
        sg = [sb("sg%d" % i, [128, 512], F32) for i in range(2)]
        sg_b = [P.buf("sg%d" % i) for i in range(2)]
        c32 = sb("c32", [128, 4, 512], F32)
        c32_b = [P.buf("c32_%d" % i) for i in range(4)]
        cbf = sb("cbf", [128, 4, 512], BF)
        cbf_b = [P.buf("cbf%d" % i) for i in range(4)]
        csq = sb("csq", [128, 4, 512], BF)
        csq_b = [P.buf("csq%d" % i) for i in range(4)]
        mean = sb("mean", [128, 512], F32)
        var = sb("var", [128, 512], F32)
        stat_b = P.buf("stat")
        an = [sb("an%d" % i, [128, 512], F32) for i in range(2)]
        an_b = [P.buf("an%d" % i) for i in range(2)]
        uTd = sb("uTD", [128, 8, 512], BF)
        uTd_b = P.buf("uTD")
        xodcnt = [0]
        xod = [sb("xoD%d" % i, [128, 1024], F32) for i in range(3)]
        xod_b = [P.buf("xoD%d" % i) for i in range(3)]
        def D_xa(ti):
            t0 = ti * 512
            for sub in range(4):
                xprep_a(x1.ap()[t0 + sub * 128:t0 + (sub + 1) * 128, :], sub, c_nw1)

        def D_xb(ti):
            for sub in range(4):
                xprep_b(sub, hTd, hTd_b, sub * 128, 7 if sub % 2 == 0 else 6)

        def D_z(ti):
            for zc in range(8):
                bank = zc % 2
                in_proj(w1, w1_b, 1536 + zc * 128, hTd, hTd_b, bank)
                P.add("act", lambda e, zc=zc, bank=bank: e.activation(out=szd[:, zc, :], in_=psum[bank][:], func=AF.Silu),
                      reads=[psb[bank]], writes=[szd_b[zc]])

        def D_uc(ti):
            for gi_ in range(4):
                p = POOLS[gi_]
                pb = gi_
                ub = 2 + gi_ % 2
                in_proj(w1, w1_b, gi_ * 128, hTd, hTd_b, ub)
                P.add("act", lambda e, gi_=gi_, ub=ub: e.copy(out=uce[:, gi_, 16:528], in_=psum[ub][:]), reads=[psb[ub]], writes=[uce_b[gi_]])
                E = uce[:, gi_, :]
                P.add("dve", lambda e, E=E: e.tensor_tensor(out=sA[:, 1:528], in0=E[:, 1:528], in1=E[:, 0:527], op=ALU.add),
                      reads=[uce_b[gi_]], writes=[sAB_b])
                res = sA
                if p >= 4:
                    P.add("dve", lambda e: e.tensor_tensor(out=sB[:, 3:528], in0=sA[:, 3:528], in1=sA[:, 1:526], op=ALU.add),
                          reads=[sAB_b], writes=[sAB_b])
                    res = sB
                if p >= 8:
                    P.add("dve", lambda e: e.tensor_tensor(out=sA[:, 7:528], in0=sB[:, 7:528], in1=sB[:, 3:524], op=ALU.add),
                          reads=[sAB_b], writes=[sAB_b])
                    res = sA
                if p >= 16:
                    P.add("dve", lambda e: e.tensor_tensor(out=sB[:, 15:528], in0=sA[:, 15:528], in1=sA[:, 7:520], op=ALU.add),
                          reads=[sAB_b], writes=[sAB_b])
                    res = sB
                P.add("dve", lambda e, res=res, E=E, pb=pb, p=p: e.scalar_tensor_tensor(out=pooled[pb][:], in0=res[:, 16:528], scalar=1.0 / p,
                                                                                    in1=E[:, 16:528], op0=ALU.mult, op1=ALU.subtract),
                      reads=[sAB_b, uce_b[gi_]], writes=[pooled_b[pb]])
                if ti == 0:
                    P.add("dve", lambda e, res=res, gi_=gi_: e.tensor_tensor(out=res[:, 0:16], in0=res[:, 16:32], in1=c_ci[:, gi_, :], op=ALU.mult),
                          reads=[sAB_b, cD], writes=[sAB_b])
                    P.add("dve", lambda e, res=res, E=E, pb=pb: e.tensor_tensor(out=pooled[pb][:, 0:16], in0=res[:, 0:16], in1=E[:, 16:32],
                                                                              op=ALU.subtract),
                          reads=[sAB_b, uce_b[gi_]], writes=[pooled_b[pb]])
                P.add("dve", lambda e, gi_=gi_: e.tensor_copy(out=uce[:, gi_, 0:16], in_=uce[:, gi_, 512:528]), reads=[uce_b[gi_]],
                      writes=[uce_b[gi_]])

        def D_pool(ti):
            for gi_ in range(4):
                pb = gi_
                P.add("pe", lambda e, gi_=gi_, pb=pb: e.matmul(psum[3][:], lhsT=pw[:, gi_, :], rhs=pooled[pb][:], start=True, stop=True),
                      reads=[pw_b, pooled_b[pb]], writes=[psb[3]])
                P.add("dve", lambda e, gi_=gi_: e.scalar_tensor_tensor(out=uTd[:, gi_, :], in0=psum[3][:], scalar=c_ps[:, gi_:gi_ + 1],
                                                                     in1=szd[:, gi_, :], op0=ALU.mult, op1=ALU.mult),
                      reads=[psb[3], cD, szd_b[gi_]], writes=[uTd_b])

        def D_glu(ti):
            for cc in range(4):
                sgi = cc % 2
                ba, bg_ = (4, 5) if cc % 2 == 0 else (2, 3)
                in_proj(w1, w1_b, 512 + cc * 128, hTd, hTd_b, ba)
                in_proj(w1, w1_b, 1024 + cc * 128, hTd, hTd_b, bg_)
                P.add("act", lambda e, sgi=sgi, bg_=bg_: e.activation(out=sg[sgi][:], in_=psum[bg_][:], func=AF.Sigmoid), reads=[psb[bg_]],
                      writes=[sg_b[sgi]])
                P.add("dve", lambda e, sgi=sgi, cc=cc, ba=ba: e.tensor_tensor(out=gle[:, cc, 32:544], in0=psum[ba][:], in1=sg[sgi][:], op=ALU.mult),
                      reads=[psb[ba], sg_b[sgi]], writes=[gle_b[cc]])

        def D_conv(ti):
            for cc in range(4):
                cb = 6 if cc % 2 == 0 else 3
                for k in range(NCONV):
                    P.add("pe", lambda e, cc=cc, k=k, cb=cb: e.matmul(psum[cb][:], lhsT=diag[:, cc, k, :], rhs=gle[:, cc, 2 + k:2 + k + 512],
                                                                    start=(k == 0), stop=(k == NCONV - 1)),
                          reads=[diag_b, gle_b[cc]], writes=[psb[cb]])
                P.add("dve", lambda e, cc=cc: e.tensor_copy(out=gle[:, cc, 0:32], in_=gle[:, cc, 512:544]), reads=[gle_b[cc]],
                      writes=[gle_b[cc]])
                P.add("act", lambda e, cc=cc, cb=cb: e.activation(out=c32[:, cc, :], in_=psum[cb][:], func=AF.Identity, bias=c_db[:, cc:cc + 1]),
                      reads=[psb[cb], cD], writes=[c32_b[cc]])
                P.add("act", lambda e, cc=cc, cb=cb: e.activation(out=csq[:, cc, :], in_=psum[cb][:], func=AF.Square, bias=c_db[:, cc:cc + 1]),
                      reads=[psb[cb], cD], writes=[csq_b[cc]])
                P.add("act", lambda e, cc=cc, cb=cb: e.activation(out=cbf[:, cc, :], in_=psum[cb][:], func=AF.Identity, bias=c_db[:, cc:cc + 1]),
                      reads=[psb[cb], cD], writes=[cbf_b[cc]])

        def D_ln(ti):
            for cc in range(4):
                P.add("pe", lambda e, cc=cc: e.matmul(psum[0][:], lhsT=onesall[:], rhs=cbf[:, cc, :], start=(cc == 0), stop=(cc == 3)),
                      reads=[cbf_b[cc], b_const], writes=[psb[0]])
            for cc in range(4):
                P.add("pe", lambda e, cc=cc: e.matmul(psum[1][:], lhsT=onesall[:], rhs=csq[:, cc, :], start=(cc == 0), stop=(cc == 3)),
                      reads=[csq_b[cc], b_const], writes=[psb[1]])
            P.add("dve", lambda e: e.tensor_scalar(out=mean[:], in0=psum[0][:], scalar1=1.0 / 512, scalar2=None, op0=ALU.mult),
                  reads=[psb[0]], writes=[stat_b])
            P.add("dve", lambda e: e.tensor_tensor(out=var[:], in0=mean[:], in1=mean[:], op=ALU.mult), reads=[stat_b], writes=[stat_b])
            P.add("dve", lambda e: e.scalar_tensor_tensor(out=var[:], in0=psum[1][:], scalar=1.0 / 512, in1=var[:], op0=ALU.mult,
                                                        op1=ALU.subtract), reads=[psb[1], stat_b], writes=[stat_b])
            P.add("act", lambda e: e.activation(out=var[:], in_=var[:], func=AF.Ln, bias=c_eps[:, 0:1]), reads=[stat_b, b_const],
                  writes=[stat_b])
            P.add("act", lambda e: e.activation(out=var[:], in_=var[:], func=AF.Exp, scale=-0.5), reads=[stat_b], writes=[stat_b])
            anl = (an[0], an[1], sg[0], sg[1])
            anl_b = (an_b[0], an_b[1], sg_b[0], sg_b[1])
            for cc in range(4):
                P.add("dve", lambda e, cc=cc: e.tensor_tensor(out=anl[cc][:], in0=c32[:, cc, :], in1=mean[:], op=ALU.subtract),
                      reads=[c32_b[cc], stat_b], writes=[anl_b[cc]])
                P.add("dve", lambda e, cc=cc: e.tensor_tensor(out=anl[cc][:], in0=anl[cc][:], in1=var[:], op=ALU.mult),
                      reads=[anl_b[cc], stat_b], writes=[anl_b[cc]])
            for cc in range(4):
                P.add("act", lambda e, cc=cc: e.activation(out=anl[cc][:], in_=anl[cc][:], func=AF.Silu, scale=c_lw[:, cc:cc + 1],
                                                         bias=c_lb[:, cc:cc + 1]), reads=[anl_b[cc], cD], writes=[anl_b[cc]])
            for cc in range(4):
                P.add("dve", lambda e, cc=cc: e.tensor_tensor(out=uTd[:, 4 + cc, :], in0=anl[cc][:], in1=szd[:, 4 + cc, :], op=ALU.mult),
                      reads=[anl_b[cc], szd_b[4 + cc]], writes=[uTd_b])

        def D_out(ti):
            if dbg:
                P.add("sp", lambda e: e.dma_start(out=dbgD.ap()[ti], in_=uTd[:]), reads=[uTd_b], dma="dbgD")
            out_proj(uTd, uTd_b, wo1, wo1_b, xod, xod_b, x1, out_t, ti * 512, "xoD", (0, 1), xodcnt)

        NT = 16
        D_xa(0)
        D_xb(0)
        D_xa(1)
        D_uc(0)
        D_glu(0)
        D_z(0)
        D_xb(1)
        D_pool(0)
        D_conv(0)
        for ti in range(NT):
            if ti + 2 < NT:
                D_xa(ti + 2)
            D_ln(ti)
            if ti + 1 < NT:
                D_uc(ti + 1)
                D_glu(ti + 1)
            D_out(ti)
            if ti + 1 < NT:
                D_z(ti + 1)
                if ti + 2 < NT:
                    D_xb(ti + 2)
                D_pool(ti + 1)
                D_conv(ti + 1)
        final_dmas += ["xoD0", "xoD1", "xoD2"]
        P.barrier()
        cur[0].close()
        cur[0] = glob_stack


    P.emit(final_dmas)
    return nc


def make_inputs(inp, b):
    bf = ml_dtypes.bfloat16
    f32 = np.float32
    m = np.arange(128)
    mm = m % 64
    inv_freq = np.power(f32(500000.0), -(np.arange(8, dtype=f32) / f32(8))).astype(f32)
    invf = np.where(mm < 16, inv_freq[mm % 8], f32(0)).astype(f32).reshape(128, 1)
    ssign = np.where(mm < 8, -1.0, np.where(mm < 16, 1.0, 0.0)).astype(f32).reshape(128, 1)
    rperm = np.zeros((128, 128), f32)
    for o in range(128):
        if o % 64 < 8:
            rperm[o + 8, o] = 1.0
        elif o % 64 < 16:
            rperm[o - 8, o] = 1.0
    onesblk = np.zeros((128, 128), f32)
    onesblk[:64, :64] = 1.0
    onesblk[64:, 64:] = 1.0
    kk = np.arange(128)[:, None]
    qq = np.arange(128)[None, :]
    mprev = (kk >= qq).astype(f32)
    mcur = (kk <= qq).astype(f32)
    reg = np.stack([mprev, mcur], 1)
    first = np.stack([np.zeros_like(mprev), mcur], 1)
    mask = np.stack([np.concatenate([reg, reg], 1).reshape(128, 512), np.concatenate([first, first], 1).reshape(128, 512)], 1)
    cinv = np.zeros((128, 4, 16), f32)
    for gi, p in enumerate(POOLS):
        cinv[:, gi, :] = 1.0 / np.minimum(np.arange(16) + 1, p)
    d = {
        "x": np.ascontiguousarray(inp["x"][b]),
        "pos": np.ascontiguousarray(inp["positions"][b].reshape(1, S)),
        "w_in0": np.ascontiguousarray(inp["e_w_in"][0]),
        "nw0": np.ascontiguousarray(inp["e_norm_w"][0].reshape(1, D)),
        "qkw": np.ascontiguousarray(np.stack([np.tile(inp["e_q_norm_w"][0], 2), np.tile(inp["e_k_norm_w"][0], 2)], 1)),
        "invf": invf, "ssign": ssign,
        "ident": np.eye(128, dtype=f32).astype(bf), "onesblk": onesblk.astype(bf), "rperm": rperm.astype(bf),
        "mask": mask.astype(bf),
        "convw0": np.ascontiguousarray(inp["e_conv_w"][0].reshape(3, 4, 128).transpose(2, 1, 0)),
        "w_out0": np.ascontiguousarray(inp["e_w_out"][0]),
        "nw1": np.ascontiguousarray(inp["o_norm_w"][0].reshape(1, D)),
        "w_in1": np.ascontiguousarray(inp["o_w_in"][0]),
        "w_out1": np.ascontiguousarray(inp["o_w_out"][0]),
        "poolw": np.ascontiguousarray(inp["o_pool_w"][0].transpose(1, 0, 2)),
        "pscale": np.ascontiguousarray(inp["o_pool_scale"][0].reshape(4, 128).T),
        "dconvw": np.ascontiguousarray(inp["o_dconv_w"][0].reshape(NCONV, 4, 128).transpose(2, 1, 0)),
        "dconvb": np.ascontiguousarray(inp["o_dconv_b"][0].reshape(4, 128).T),
        "lnw": np.ascontiguousarray(inp["o_ln_w"][0].reshape(4, 128).T),
        "lnb": np.ascontiguousarray(inp["o_ln_b"][0].reshape(4, 128).T),
        "cinv": cinv,
        "onesall": np.ones((128, 128), f32).astype(bf),
    }
    return {k: np.ascontiguousarray(v.astype(np.float32) if v.dtype == np.float64 else v) for k, v in d.items()}


_NC = {}


def kernel(**inputs):
    inp = {k: np.asarray(v) for k, v in inputs.items()}
    if "full" not in _NC:
        _NC["full"] = build("ABCD")
    nc = _NC["full"]
    in_maps = [make_inputs(inp, b) for b in range(8)]
    res = run_bass_kernel_spmd(nc, in_maps, core_ids=list(range(8)))
    return np.stack([np.asarray(r["out"], dtype=np.float32).reshape(S, D) for r in res.results], 0)
```

```python
import math
from contextlib import ExitStack
import numpy as np
import ml_dtypes
import concourse.bass as bass
import concourse.mybir as mybir
from concourse.bass_utils import run_bass_kernel_spmd

F32 = mybir.dt.float32
BF = mybir.dt.bfloat16
I32 = mybir.dt.int32
AF = mybir.ActivationFunctionType
ALU = mybir.AluOpType

S = 8192
D = 1024
EPS = 1e-6
GROUPS = ((128, 1), (512, 4), (2048, 16))
POOLS = (2, 4, 8, 16)
NCONV = 31
MAGIC = 12582912.0
TWO_PI = 2.0 * math.pi


def _cw_consts():
    c1 = np.float32(6.28125)
    rem = np.float64(TWO_PI) - np.float64(c1)
    c2 = np.float32(rem)
    c2 = np.frombuffer(np.uint32(np.frombuffer(np.float32(c2).tobytes(), np.uint32)[0] & 0xFFFFF000).tobytes(), np.float32)[0]
    c3 = np.float32(rem - np.float64(c2))
    return float(c1), float(c2), float(c3)


class Buf:
    __slots__ = ("name", "w", "r", "mw")

    def __init__(self, name):
        self.name = name
        self.w = {}
        self.r = {}
        self.mw = set()


class Prog:
    ENGS = ("pe", "act", "dve", "pool", "sp")

    def __init__(self, nc):
        self.nc = nc
        self.ops = []
        self.dma_cnt = {}
        self.dma_last = {}
        self.bufs = []
        self.base_w = {}

    def buf(self, name):
        b = Buf(name)
        b.w = dict(self.base_w)
        self.bufs.append(b)
        return b

    def barrier(self):
        allb = list(self.bufs)
        for k, i in self.dma_last.items():
            pass
        idx = self.add("sp", lambda e: e.nop(), reads=(), writes=allb)
        self.ops[idx][2].update(self.dma_last.values())
        self.ops[idx][2].discard(idx)
        self.base_w = {("e", "sp"): idx}
        return idx

    def add(self, eng, fn, reads=(), writes=(), dma=None, merge=False):
        idx = len(self.ops)
        deps = set()
        for b in reads:
            deps.update(b.w.values())
        for b in writes:
            if merge:
                deps.update(v for v in b.w.values() if v not in b.mw)
            else:
                deps.update(b.w.values())
            deps.update(b.r.values())
        val = None
        if dma is not None:
            if dma in self.dma_last:
                deps.add(self.dma_last[dma])
            self.dma_last[dma] = idx
            self.dma_cnt[dma] = self.dma_cnt.get(dma, 0) + 1
            val = 16 * self.dma_cnt[dma]
        self.ops.append((eng, fn, deps, dma, val))
        key = ("d", dma) if dma is not None else ("e", eng)
        for b in writes:
            if merge:
                b.w[key] = idx
                b.mw.add(idx)
            else:
                b.w = {key: idx}
                b.r = {}
                b.mw = set()
        for b in reads:
            if b not in writes:
                b.r[key] = idx
        return idx

    def emit(self, final_dmas=()):
        nc = self.nc
        ops = self.ops

        def skip(p, o):
            return p[3] is None and o[3] is None and p[0] == "pe" and o[0] == "pe"

        need = set()
        for o in ops:
            for d in o[2]:
                if not skip(ops[d], o):
                    need.add(d)
        esem = {e: nc.alloc_semaphore(name="sem_" + e) for e in self.ENGS}
        dsem = {k: nc.alloc_semaphore(name="dsem_" + str(k)) for k in self.dma_cnt}
        val = {}
        cnt = {e: 0 for e in self.ENGS}
        for i, o in enumerate(ops):
            if o[3] is not None:
                val[i] = ("d" + str(o[3]), dsem[o[3]], o[4], 16)
            elif i in need:
                cnt[o[0]] += 1
                val[i] = ("e" + o[0], esem[o[0]], cnt[o[0]], 1)
        finals = [(("d" + str(k)), dsem[k], 16 * self.dma_cnt[k]) for k in final_dmas]

        def run(name, e):
            seen = {}
            for i, o in enumerate(ops):
                if o[0] != name:
                    continue
                for d in sorted(o[2]):
                    if skip(ops[d], o):
                        continue
                    sk, sem, v, _ = val[d]
                    if seen.get(sk, 0) < v:
                        e.wait_ge(sem, v)
                        seen[sk] = v
                ins = o[1](e)
                if i in val:
                    ins.then_inc(val[i][1], val[i][3])
            if name == "sp":
                for sk, sem, v in finals:
                    if seen.get(sk, 0) < v:
                        e.wait_ge(sem, v)

        with nc.Block() as block:
            @block.tensor
            def _(e):
                run("pe", e)

            @block.scalar
            def _(e):
                run("act", e)

            @block.vector
            def _(e):
                run("dve", e)

            @block.gpsimd
            def _(e):
                run("pool", e)

            @block.sync
            def _(e):
                run("sp", e)


def build(phases="ABCD", dbg=False):
    nc = bass.Bass("TRN2", target_bir_lowering=False)
    P = Prog(nc)

    def din(name, shape, dt):
        return nc.dram_tensor(name, list(shape), dt, kind="ExternalInput")

    x_t = din("x", [S, D], F32)
    pos_t = din("pos", [1, S], I32)
    w_in0 = din("w_in0", [D, 7168], F32)
    nw0 = din("nw0", [1, D], F32)
    qkw = din("qkw", [128, 2], F32)
    invf = din("invf", [128, 1], F32)
    ssign = din("ssign", [128, 1], F32)
    ident_d = din("ident", [128, 128], BF)
    onesblk_d = din("onesblk", [128, 128], BF)
    rperm_d = din("rperm", [128, 128], BF)
    mask_d = din("mask", [128, 2, 512], BF)
    convw0 = din("convw0", [128, 4, 3], F32)
    w_out0 = din("w_out0", [D, D], F32)
    nw1 = din("nw1", [1, D], F32)
    w_in1 = din("w_in1", [D, 2560], F32)
    w_out1 = din("w_out1", [D, D], F32)
    poolw = din("poolw", [128, 4, 128], F32)
    pscale = din("pscale", [128, 4], F32)
    dconvw = din("dconvw", [128, 4, NCONV], F32)
    dconvb = din("dconvb", [128, 4], F32)
    lnw = din("lnw", [128, 4], F32)
    lnb = din("lnb", [128, 4], F32)
    cinv = din("cinv", [128, 4, 16], F32)
    onesall_d = din("onesall", [128, 128], BF)

    out_t = nc.dram_tensor("out", [S, D], F32, kind="ExternalOutput")
    kind_s = "ExternalOutput" if dbg else "Internal"
    qT = nc.dram_tensor("qT", [3, 4, 128, S], BF, kind=kind_s)
    kT = nc.dram_tensor("kT", [3, 4, 128, S], BF, kind=kind_s)
    vx = nc.dram_tensor("vx", [S, 24, 65], BF, kind=kind_s)
    og = nc.dram_tensor("og", [3, S, 520], F32, kind=kind_s)
    x1 = nc.dram_tensor("x1", [S, D], F32, kind=kind_s)
    if dbg:
        dbgC = nc.dram_tensor("dbgC", [16, 128, 8, 512], BF, kind="ExternalOutput")
        dbgD = nc.dram_tensor("dbgD", [16, 128, 8, 512], BF, kind="ExternalOutput")

    cur = [ExitStack()]

    def sb(name, shape, dt):
        return cur[0].enter_context(nc.sbuf_tensor("s_" + name, list(shape), dt))

    psum = [nc.alloc_psum_tensor("ps%d" % i, [128, 512], F32) for i in range(8)]
    psb = [P.buf("ps%d" % i) for i in range(8)]

    ident = sb("ident", [128, 128], BF)
    onesblk = sb("onesblk", [128, 128], BF)
    rperm = sb("rperm", [128, 128], BF)
    onesall = sb("onesall", [128, 128], BF)
    c_qkw = sb("c_qkw", [128, 2], F32)
    c_invf = sb("c_invf", [128, 1], F32)
    c_ssign = sb("c_ssign", [128, 1], F32)
    c_nw0 = sb("c_nw0", [128, D], F32)
    c_nw1 = c_nw0
    b_const = P.buf("const")
    c_eps = sb("c_eps", [128, 1], F32)
    P.add("pool", lambda e: e.memset(c_eps[:], EPS), writes=[b_const])
    for i, (dst, src) in enumerate([(ident, ident_d), (onesblk, onesblk_d), (rperm, rperm_d), (onesall, onesall_d),
                                    (c_qkw, qkw), (c_invf, invf), (c_ssign, ssign)]):
        P.add("sp", (lambda e, d=dst, s=src: e.dma_start(out=d[:], in_=s.ap())), writes=[b_const], dma="const")
    b_nw = P.buf("nw")
    P.add("sp", lambda e: e.dma_start(out=c_nw0[:], in_=bass.AP(nw0, 0, [[0, 128], [1, D]])), writes=[b_nw], dma="const")

    final_dmas = []

    xt = [sb("xt%d" % i, [128, D], F32) for i in range(4)]
    xt_b = [P.buf("xt%d" % i) for i in range(4)]
    hb = [sb("hb%d" % i, [128, D], BF) for i in range(4)]
    hb_b = [P.buf("hb%d" % i) for i in range(4)]
    st = [sb("st%d" % i, [128, 4], F32) for i in range(4)]
    st_b = [P.buf("st%d" % i) for i in range(4)]
    cnt_prep = [0]

    def xprep_a(src_ap, xi, nwt):
        hi = xi
        P.add("sp", lambda e: e.dma_start(out=xt[xi][:], in_=src_ap), writes=[xt_b[xi]], dma="xt%d" % xi)
        P.add("act", lambda e: e.activation(out=hb[hi][:], in_=xt[xi][:], func=AF.Square, accum_out=st[hi][:, 0:1]),
              reads=[xt_b[xi]], writes=[hb_b[hi], st_b[hi]])
        P.add("act", lambda e: e.activation(out=st[hi][:, 1:2], in_=st[hi][:, 0:1], func=AF.Ln, scale=1.0 / D, bias=c_eps[:, 0:1]),
              reads=[st_b[hi], b_const], writes=[st_b[hi]])
        P.add("act", lambda e: e.activation(out=st[hi][:, 2:3], in_=st[hi][:, 1:2], func=AF.Exp, scale=-0.5), reads=[st_b[hi]],
              writes=[st_b[hi]])
        P.add("dve", lambda e: e.scalar_tensor_tensor(out=hb[hi][:], in0=xt[xi][:], scalar=st[hi][:, 2:3], in1=nwt[:], op0=ALU.mult,
                                                      op1=ALU.mult), reads=[xt_b[xi], st_b[hi], b_nw], writes=[hb_b[hi]])

    def xprep_b(xi, hT, hT_b, col0, pbank):
        hi = xi
        i = cnt_prep[0]
        cnt_prep[0] += 1
        pv = psum[pbank][:].bitcast(BF)
        for k in range(8):
            P.add("pe", lambda e, k=k: e.transpose(out=pv[:, k * 128:(k + 1) * 128], in_=hb[hi][:, k * 128:(k + 1) * 128], identity=ident[:]),
                  reads=[hb_b[hi], b_const], writes=[psb[pbank]])
        src = pv.rearrange("p (k t) -> p k t", k=8)
        if i % 2 == 0:
            P.add("act", lambda e: e.copy(out=hT[:, :, col0:col0 + 128], in_=src), reads=[psb[pbank]], writes=[hT_b])
        else:
            P.add("dve", lambda e: e.tensor_copy(out=hT[:, :, col0:col0 + 128], in_=src), reads=[psb[pbank]], writes=[hT_b])

    def xprep(src_ap, xi, hT, hT_b, col0, pbank, nwt):
        xprep_a(src_ap, xi, nwt)
        xprep_b(xi, hT, hT_b, col0, pbank)

    lw_cnt = [0]

    def load_weight(dst, dst_b, w_dram, col0, ncols, *_unused):
        for c0 in range(0, ncols, 1280):
            cw = min(1280, ncols - c0)
            src = w_dram.ap()[:, col0 + c0:col0 + c0 + cw].rearrange("(k p) f -> p k f", p=128)
            key = "lw%d" % (lw_cnt[0] % 6)
            lw_cnt[0] += 1
            P.add("pool", lambda e, c0=c0, cw=cw, src=src: e.dma_start(out=dst[:, :, c0:c0 + cw], in_=src), writes=[dst_b], dma=key,
                  merge=True)

    wst = wst_b = None

    if "A" in phases:
        glob_stack = cur[0]
        cur[0] = ExitStack()
        wv = sb("wv", [128, 8, 1536], BF)
        wv_b = P.buf("wv")
        load_weight(wv, wv_b, w_in0, 3072, 1536)

        hT = sb("hT", [128, 8, 2048], BF)
        hT_b = P.buf("hT")
        tabC = sb("tabC", [128, 2048], F32)
        tabS = sb("tabS", [128, 2048], F32)
        tab_b = P.buf("tab")
        tmpA = sb("tmpA", [128, 2048], F32)
        tmpB = sb("tmpB", [128, 2048], F32)
        posi = sb("posi", [128, 2048], I32)
        tmp_b = P.buf("tmp")
        wch = [sb("wch%d" % i, [128, 8, 128], BF) for i in range(3)]
        wch_b = [P.buf("wch%d" % i) for i in range(3)]
        qw = [sb("qw%d" % i, [128, 512], BF) for i in range(2)]
        sq = [sb("sq%d" % i, [128, 512], BF) for i in range(2)]
        sd = [sb("sd%d" % i, [128, 512], F32) for i in range(2)]
        t1 = [sb("t1_%d" % i, [128, 512], F32) for i in range(2)]
        t2 = [sb("t2_%d" % i, [128, 512], F32) for i in range(2)]
        qw_b = [P.buf("qw%d" % i) for i in range(2)]
        sq_b = [P.buf("sq%d" % i) for i in range(2)]
        sd_b = [P.buf("sd%d" % i) for i in range(2)]
        t1_b = [P.buf("t1_%d" % i) for i in range(2)]
        t2_b = [P.buf("t2_%d" % i) for i in range(2)]
        qst = [sb("qst%d" % i, [128, 2048], BF) for i in range(2)]
        qst_b = [P.buf("qst%d" % i) for i in range(2)]
        vst = [sb("vst%d" % i, [128, 24, 65], BF) for i in range(2)]
        vst_b = [P.buf("vst%d" % i) for i in range(2)]
        for i in range(2):
            P.add("pool", lambda e, i=i: e.memset(vst[i][:, :, 64:65], 1.0), writes=[vst_b[i]])
        c1, c2, c3 = _cw_consts()
        PI_LO = 3.1415925

        wch_it = 0
        unit = 0
        vcount = 0
        for s in range(4):
            t0 = s * 2048
            for j in range(4):
                xprep_a(x_t.ap()[t0 + j * 128:t0 + (j + 1) * 128, :], j % 4, c_nw0)
            for j in range(16):
                xprep_b(j % 4, hT, hT_b, j * 128, 6 + (j % 2))
                if j + 4 < 16:
                    xprep_a(x_t.ap()[t0 + (j + 4) * 128:t0 + (j + 5) * 128, :], (j + 4) % 4, c_nw0)

            P.add("sp", lambda e, t0=t0: e.dma_start(out=posi[:], in_=bass.AP(pos_t, t0, [[0, 128], [1, 2048]])), writes=[tmp_b],
                  dma="posi")
            P.add("dve", lambda e: e.tensor_copy(out=tmpA[:], in_=posi[:]), reads=[tmp_b], writes=[tmp_b])
            P.add("dve", lambda e: e.tensor_scalar(out=tmpA[:], in0=tmpA[:], scalar1=c_invf[:, 0:1], scalar2=None, op0=ALU.mult),
                  reads=[tmp_b, b_const], writes=[tmp_b])
            P.add("dve", lambda e: e.tensor_scalar(out=tmpB[:], in0=tmpA[:], scalar1=1.0 / TWO_PI, scalar2=MAGIC, op0=ALU.mult, op1=ALU.add),
                  reads=[tmp_b], writes=[tmp_b])
            P.add("dve", lambda e: e.tensor_scalar(out=tmpB[:], in0=tmpB[:], scalar1=-MAGIC, scalar2=None, op0=ALU.add),
                  reads=[tmp_b], writes=[tmp_b])
            for cc_ in (c1, c2, c3):
                P.add("dve", lambda e, cc_=cc_: e.scalar_tensor_tensor(out=tmpA[:], in0=tmpB[:], scalar=-cc_, in1=tmpA[:], op0=ALU.mult,
                                                                     op1=ALU.add), reads=[tmp_b], writes=[tmp_b])
            P.add("dve", lambda e: e.tensor_scalar(out=tmpA[:], in0=tmpA[:], scalar1=PI_LO, scalar2=-PI_LO, op0=ALU.min, op1=ALU.max),
                  reads=[tmp_b], writes=[tmp_b])
            P.add("act", lambda e: e.activation(out=tabS[:], in_=tmpA[:], func=AF.Sin), reads=[tmp_b], writes=[tab_b])
            P.add("dve", lambda e: e.tensor_scalar(out=tabS[:], in0=tabS[:], scalar1=c_ssign[:, 0:1], scalar2=None, op0=ALU.mult),
                  reads=[tab_b, b_const], writes=[tab_b])
            P.add("dve", lambda e: e.tensor_scalar(out=tmpA[:], in0=tmpA[:], scalar1=math.pi / 2, scalar2=None, op0=ALU.add),
                  reads=[tmp_b], writes=[tmp_b])
            P.add("dve", lambda e: e.tensor_scalar(out=tmpB[:], in0=tmpA[:], scalar1=math.pi, scalar2=-TWO_PI, op0=ALU.is_gt, op1=ALU.mult),
                  reads=[tmp_b], writes=[tmp_b])
            P.add("dve", lambda e: e.tensor_tensor(out=tmpA[:], in0=tmpA[:], in1=tmpB[:], op=ALU.add), reads=[tmp_b], writes=[tmp_b])
            P.add("dve", lambda e: e.tensor_scalar(out=tmpA[:], in0=tmpA[:], scalar1=PI_LO, scalar2=-PI_LO, op0=ALU.min, op1=ALU.max),
                  reads=[tmp_b], writes=[tmp_b])
            P.add("act", lambda e: e.activation(out=tabC[:], in_=tmpA[:], func=AF.Sin), reads=[tmp_b, tab_b], writes=[tab_b])

            for j in range(16):
                vb = vcount % 2
                vcount += 1
                for g in range(3):
                    bank = 7 if (j * 3 + g) % 2 == 0 else 6
                    for k in range(8):
                        P.add("pe", lambda e, k=k, j=j, g=g, bank=bank: e.matmul(psum[bank][:], lhsT=hT[:, k, j * 128:(j + 1) * 128],
                                                                               rhs=wv[:, k, g * 512:(g + 1) * 512], start=(k == 0), stop=(k == 7)),
                              reads=[hT_b, wv_b], writes=[psb[bank]])
                    P.add("act", lambda e, vb=vb, g=g, bank=bank: e.copy(out=vst[vb][:, g * 8:(g + 1) * 8, 0:64],
                                                                       in_=psum[bank][:].rearrange("p (h c) -> p h c", h=8)),
                          reads=[psb[bank]], writes=[vst_b[vb]])
                dst = vx.ap()[t0 + j * 128:t0 + (j + 1) * 128, :, :]
                P.add("pool", lambda e, vb=vb, dst=dst: e.dma_start(out=dst, in_=vst[vb][:]), reads=[vst_b[vb]], dma="vst%d" % vb)

            def wload(ci_g):
                ci = ci_g % 24
                isk = ci >= 12
                g = (ci % 12) // 4
                cc = ci % 4
                col0 = (1536 if isk else 0) + g * 512 + cc * 128
                wb = ci_g % 3
                src = w_in0.ap()[:, col0:col0 + 128].rearrange("(k p) f -> p k f", p=128)
                P.add("pool", lambda e: e.dma_start(out=wch[wb][:], in_=src), writes=[wch_b[wb]], dma="wch%d" % wb)

            def stage1(ci, qd, u):
                isk = ci >= 12
                wb = (s * 24 + ci) % 3
                b = u % 2
                acc = b
                cs = slice(qd * 512, (qd + 1) * 512)
                for k in range(8):
                    P.add("pe", lambda e, k=k: e.matmul(psum[acc][:], lhsT=wch[wb][:, k, :], rhs=hT[:, k, cs], start=(k == 0), stop=(k == 7)),
                          reads=[wch_b[wb], hT_b], writes=[psb[acc]])
                P.add("act", lambda e: e.activation(out=qw[b][:], in_=psum[acc][:], func=AF.Copy,
                                                    scale=c_qkw[:, (1 if isk else 0):(2 if isk else 1)]),
                      reads=[psb[acc], b_const], writes=[qw_b[b]])
                P.add("act", lambda e: e.activation(out=sq[b][:], in_=psum[acc][:], func=AF.Square), reads=[psb[acc]], writes=[sq_b[b]])

            def stage2(ci, qd, u):
                isk = ci >= 12
                g = (ci % 12) // 4
                cc = ci % 4
                d = GROUPS[g][1]
                b = u % 2
                ssb = 2 + b
                rqb = 4 + b
                sbi = ci % 2
                cs = slice(qd * 512, (qd + 1) * 512)
                P.add("pe", lambda e: e.matmul(psum[ssb][:], lhsT=onesblk[:], rhs=sq[b][:], start=True, stop=True),
                      reads=[sq_b[b], b_const], writes=[psb[ssb]])
                P.add("pe", lambda e: e.matmul(psum[rqb][:], lhsT=rperm[:], rhs=qw[b][:], start=True, stop=True),
                      reads=[qw_b[b], b_const], writes=[psb[rqb]])
                P.add("act", lambda e: e.activation(out=sd[b][:], in_=psum[ssb][:], func=AF.Ln, scale=1.0 / 64, bias=c_eps[:, 0:1]),
                      reads=[psb[ssb], b_const], writes=[sd_b[b]])
                P.add("act", lambda e: e.activation(out=sd[b][:], in_=sd[b][:], func=AF.Exp, scale=-0.5), reads=[sd_b[b]], writes=[sd_b[b]])
                P.add("pool", lambda e: e.tensor_tensor(out=t1[b][:], in0=qw[b][:], in1=tabC[:, cs], op=ALU.mult),
                      reads=[qw_b[b], tab_b], writes=[t1_b[b]])
                P.add("dve", lambda e: e.tensor_tensor(out=t2[b][:], in0=psum[rqb][:], in1=tabS[:, cs], op=ALU.mult),
                      reads=[psb[rqb], tab_b], writes=[t2_b[b]])
                P.add("dve", lambda e: e.tensor_tensor(out=t2[b][:], in0=t2[b][:], in1=t1[b][:], op=ALU.add),
                      reads=[t1_b[b], t2_b[b]], writes=[t2_b[b]])
                nq = 512 // d
                dstv = qst[sbi][:].rearrange("f (r n) -> f r n", r=d)[:, :, qd * nq:(qd + 1) * nq]
                P.add("dve", lambda e: e.tensor_tensor(out=dstv, in0=t2[b][:].rearrange("f (n r) -> f r n", r=d),
                                                       in1=sd[b][:].rearrange("f (n r) -> f r n", r=d), op=ALU.mult),
                      reads=[t2_b[b], sd_b[b]], writes=[qst_b[sbi]])
                if qd == 3:
                    L = 2048 // d
                    tgt = kT if isk else qT
                    dst = tgt.ap()[g, cc].rearrange("f (r n) -> f r n", r=d)[:, :, s * L:(s + 1) * L]
                    P.add("sp", lambda e: e.dma_start(out=dst, in_=qst[sbi][:].rearrange("f (r n) -> f r n", r=d)),
                          reads=[qst_b[sbi]], dma="qst%d" % sbi)

            if s == 0:
                wload(0)
                wload(1)
            prev = None
            for ci in range(24):
                if s * 24 + ci + 2 < 96:
                    wload(s * 24 + ci + 2)
                for qd in range(4):
                    stage1(ci, qd, unit)
                    if prev is not None:
                        stage2(*prev)
                    prev = (ci, qd, unit)
                    unit += 1
            stage2(*prev)
        final_dmas += ["vst0", "vst1", "qst0", "qst1"]
        P.barrier()
        cur[0].close()
        cur[0] = glob_stack

    def bc3(t_ap, n_inner):
        return bass.AP(t_ap.tensor, t_ap.offset, [list(t_ap.ap[0]), list(t_ap.ap[1]), [0, n_inner]])

    if "B" in phases:
        glob_stack = cur[0]
        cur[0] = ExitStack()
        maskt = sb("maskt", [128, 2, 512], BF)
        mask_b = P.buf("mask")
        P.add("sp", lambda e: e.dma_start(out=maskt[:], in_=mask_d.ap()), writes=[mask_b], dma="maskld")
        LMAX = 1024
        NJB = 3
        qTj = [sb("qTj%d" % i, [128, 4, LMAX], BF) for i in range(NJB)]
        kTa = [sb("kTa%d" % i, [128, 4, LMAX + 128], BF) for i in range(NJB)]
        kTb = [sb("kTb%d" % i, [128, 4, LMAX + 128], BF) for i in range(NJB)]
        kTh = (kTa, kTb)
        vj = [sb("vj%d" % i, [128, LMAX // 128 + 1, 520], BF) for i in range(NJB)]
        qj_b = [P.buf("qj%d" % i) for i in range(NJB)]
        kj_b = [P.buf("kj%d" % i) for i in range(NJB)]
        ka_b = [P.buf("ka%d" % i) for i in range(NJB)]
        kb_b = [P.buf("kb%d" % i) for i in range(NJB)]
        va_b = [P.buf("va%d" % i) for i in range(NJB)]
        vb_b = [P.buf("vb%d" % i) for i in range(NJB)]
        for i in range(NJB):
            P.add("pool", lambda e, i=i: e.memset(kTa[i][:], 0.0), writes=[ka_b[i]])
            P.add("pool", lambda e, i=i: e.memset(kTb[i][:], 0.0), writes=[kb_b[i]])
        kj_init = kj_b
        vj_b = [P.buf("vj%d" % i) for i in range(NJB)]
        pT = [sb("pT%d" % i, [128, 512], BF) for i in range(5)]
        pT_b = [P.buf("pT%d" % i) for i in range(5)]
        ost = [sb("ost%d" % i, [128, 8, 65], F32) for i in range(2)]
        ost_b = [P.buf("ost%d" % i) for i in range(2)]
        jobs = []
        for g, (w_, d) in enumerate(GROUPS):
            N = S // d
            L = min(LMAX, N)
            for r in range(d):
                for seg in range(N // L):
                    jobs.append((g, d, r, seg * L, L))

        def job_loads(ji):
            g, d, r, n0, L = jobs[ji]
            jb = ji % NJB
            nblk = L // 128
            qsrc = qT.ap()[g].rearrange("c f (r n) -> f c r n", r=d)[:, :, r, n0:n0 + L]
            P.add("sp", lambda e: e.dma_start(out=qTj[jb][:, :, 0:L], in_=qsrc), writes=[qj_b[jb]], dma="qj%d" % jb)
            kview = kT.ap()[g].rearrange("c f (r n) -> f c r n", r=d)
            if n0 == 0:
                P.add("pool", lambda e: e.memset(kTa[jb][0:64, :, 0:128], 0.0), writes=[ka_b[jb]])
                P.add("pool", lambda e: e.memset(kTb[jb][64:128, :, 0:128], 0.0), writes=[kb_b[jb]])
                P.add("pool", lambda e: e.memset(vj[jb][:, 0, :], 0.0), writes=[va_b[jb]])
                bb0 = 1
            else:
                bb0 = 0
            c0 = 128 * bb0
            for hh_, kt_, kb_ in ((0, kTa, ka_b), (1, kTb, kb_b)):
                ps_ = slice(hh_ * 64, (hh_ + 1) * 64)
                P.add("sp", lambda e, kt_=kt_, ps_=ps_: e.dma_start(out=kt_[jb][ps_, :, c0:128 + L],
                                                                  in_=kview[ps_, :, r, n0 - 128 + c0:n0 + L]),
                      writes=[kb_[jb]], dma="k%d_%d" % (hh_, jb))
            for pi_, b0_ in enumerate(range(bb0, nblk + 1, 5)):
                nb_ = min(5, nblk + 1 - b0_)
                off = (r + d * (n0 - 128 + 128 * b0_)) * 1560 + g * 520
                vsrc = bass.AP(vx, off, [[d * 1560, 128], [128 * d * 1560, nb_], [1, 520]])
                vb_ = va_b if pi_ == 0 else vb_b
                P.add("sp", lambda e, b0_=b0_, nb_=nb_, vsrc=vsrc: e.dma_start(out=vj[jb][:, b0_:b0_ + nb_, :], in_=vsrc),
                      writes=[vb_[jb]], dma="v%d_%d" % (pi_, jb))

        units = []
        for ji, (g, d, r, n0, L) in enumerate(jobs):
            for b in range(L // 128):
                for hp in range(4):
                    units.append((ji, b, hp))
        ucount = [0]
        ocount = [0]

        def stage1(u):
            ji, b, hp = units[u]
            g, d, r, n0, L = jobs[ji]
            jb = ji % NJB
            sbk = (0, 1, 2, 7)[u % 4]
            pp = u % 5
            for hh in range(2):
                for half in range(2):
                    P.add("pe", lambda e, hh=hh, half=half: e.matmul(
                        psum[sbk][:, (hh * 2 + half) * 128:(hh * 2 + half + 1) * 128],
                        lhsT=kTh[hh][jb][:, hp, (b + half) * 128:(b + half + 1) * 128],
                        rhs=qTj[jb][:, hp, b * 128:(b + 1) * 128], start=True, stop=True),
                        reads=[(ka_b, kb_b)[hh][jb], qj_b[jb]], writes=[psb[sbk]])
            P.add("act", lambda e: e.activation(out=pT[pp][:], in_=psum[sbk][:], func=AF.Exp, scale=0.125), reads=[psb[sbk]],
                  writes=[pT_b[pp]])
            mi = 1 if (n0 + 128 * b == 0) else 0
            P.add("dve", lambda e: e.tensor_tensor(out=pT[pp][:], in0=pT[pp][:], in1=maskt[:, mi, :], op=ALU.mult),
                  reads=[pT_b[pp], mask_b], writes=[pT_b[pp]])

        def stage2(u):
            ji, b, hp = units[u]
            g, d, r, n0, L = jobs[ji]
            jb = ji % NJB
            pp = u % 5
            oset = (u // 4) % 2
            banks = (3 + 2 * oset, 4 + 2 * oset)
            for hh in range(2):
                head = hp * 2 + hh
                ob = banks[head // 4]
                for half in range(2):
                    P.add("pe", lambda e, hh=hh, half=half, head=head, ob=ob: e.matmul(
                        psum[ob][:, (head % 4) * 65:(head % 4 + 1) * 65],
                        lhsT=pT[pp][:, (hh * 2 + half) * 128:(hh * 2 + half + 1) * 128],
                        rhs=vj[jb][:, b + half, head * 65:(head + 1) * 65], start=(half == 0), stop=(half == 1)),
                        reads=[pT_b[pp], va_b[jb], vb_b[jb]], writes=[psb[ob]])
            if hp == 3:
                evq.append((u, g, d, r, n0, b, banks))

        evq = []

        def evac(item):
            u, g, d, r, n0, b, banks = item
            oi = ocount[0] % 2
            ocount[0] += 1
            P.add("dve", lambda e: e.tensor_copy(out=ost[oi][:, 0:4, :], in_=psum[banks[0]][:, 0:260].rearrange("p (h c) -> p h c", h=4)),
                  reads=[psb[banks[0]]], writes=[ost_b[oi]])
            P.add("dve", lambda e: e.tensor_copy(out=ost[oi][:, 4:8, :], in_=psum[banks[1]][:, 0:260].rearrange("p (h c) -> p h c", h=4)),
                  reads=[psb[banks[1]]], writes=[ost_b[oi]])
            dst = bass.AP(og, (g * S + r + d * (n0 + 128 * b)) * 520, [[d * 520, 128], [1, 520]])
            P.add("pool", lambda e: e.dma_start(out=dst, in_=ost[oi][:].rearrange("p h c -> p (h c)")), reads=[ost_b[oi]],
                  dma="ost%d" % oi)

        LAG = 3
        job_loads(0)
        first_unit = {}
        for u, (ji, b, hp) in enumerate(units):
            if b == 0 and hp == 0:
                first_unit[ji] = u
        next_load = 1
        for u in range(len(units)):
            stage1(u)
            if u >= LAG:
                stage2(u - LAG)
            while evq and evq[0][0] + LAG + 2 <= u:
                evac(evq.pop(0))
            if next_load < len(jobs) and (next_load < NJB or u - LAG >= first_unit[next_load - NJB + 1] - 1):
                job_loads(next_load)
                next_load += 1
        for u in range(len(units) - LAG, len(units)):
            stage2(u)
        while evq:
            evac(evq.pop(0))
        final_dmas += ["ost0", "ost1"]
        P.barrier()
        cur[0].close()
        cur[0] = glob_stack

    def out_proj(uT, uT_b, wo, wo_b, xo, xo_b, res_t, dst_t, t0, tagkey, banks, cnt):
        it = 0
        ois = []
        for sub in range(4):
            ois.append(cnt[0] % len(xo))
            cnt[0] += 1

        def reload(sub):
            oi = ois[sub]
            rows = slice(t0 + sub * 128, t0 + (sub + 1) * 128)
            P.add("sp", lambda e: e.dma_start(out=xo[oi][:], in_=res_t.ap()[rows, :]), writes=[xo_b[oi]], dma=tagkey + "r" + str(oi))

        H = min(4, len(xo))
        for sub in range(H):
            reload(sub)
        for sub in range(4):
            oi = ois[sub]
            rows = slice(t0 + sub * 128, t0 + (sub + 1) * 128)
            for half in range(2):
                bank = banks[it % len(banks)]
                it += 1
                for cc in range(8):
                    P.add("pe", lambda e, cc=cc, sub=sub, half=half, bank=bank: e.matmul(
                        psum[bank][:], lhsT=uT[:, cc, sub * 128:(sub + 1) * 128], rhs=wo[:, cc, half * 512:(half + 1) * 512],
                        start=(cc == 0), stop=(cc == 7)), reads=[uT_b, wo_b], writes=[psb[bank]])
                P.add("dve", lambda e, half=half, bank=bank, oi=oi: e.tensor_tensor(
                    out=xo[oi][:, half * 512:(half + 1) * 512], in0=psum[bank][:], in1=xo[oi][:, half * 512:(half + 1) * 512], op=ALU.add),
                    reads=[psb[bank], xo_b[oi]], writes=[xo_b[oi]])
            P.add("pool", lambda e, oi=oi, rows=rows: e.dma_start(out=dst_t.ap()[rows, :], in_=xo[oi][:]), reads=[xo_b[oi]],
                  dma=tagkey + str(oi))
            if sub + H < 4:
                reload(sub + H)

    def in_proj(w, w_b, col0, hTt, hT_b, bank):
        for k in range(8):
            P.add("pe", lambda e, k=k: e.matmul(psum[bank][:], lhsT=w[:, k, col0:col0 + 128], rhs=hTt[:, k, :], start=(k == 0), stop=(k == 7)),
                  reads=[w_b, hT_b], writes=[psb[bank]])

    if "C" in phases:
        glob_stack = cur[0]
        cur[0] = ExitStack()
        wbz = sb("wbz", [128, 8, 2560], BF)
        wbz_b = P.buf("wbz")
        load_weight(wbz, wbz_b, w_in0, 4608, 2560)
        wo0 = sb("wo0", [128, 8, 1024], BF)
        wo0_b = P.buf("wo0")
        load_weight(wo0, wo0_b, w_out0, 0, 1024)
        c_cw = sb("c_cw", [128, 4, 3], F32)
        cc_b = P.buf("c_cw")
        P.add("sp", lambda e: e.dma_start(out=c_cw[:], in_=convw0.ap()), writes=[cc_b], dma="ccw")
        hTc2 = [sb("hTc%d" % i, [128, 8, 512], BF) for i in range(2)]
        hTc2_b = [P.buf("hTc%d" % i) for i in range(2)]
        gi = sb("gi", [128, 4, 514], F32)
        gi_b = [P.buf("gi%d" % i) for i in range(4)]
        P.add("pool", lambda e: e.memset(gi[:], 0.0), writes=gi_b)
        tmpc = [sb("tmpc%d" % i, [128, 512], F32) for i in range(2)]
        tmpc_b = [P.buf("tmpc%d" % i) for i in range(2)]
        ycv = [sb("ycv%d" % i, [128, 512], F32) for i in range(2)]
        ycv_b = [P.buf("ycv%d" % i) for i in range(2)]
        yb = sb("yb", [128, 4, 512], F32)
        yb_b = [P.buf("yb%d" % i) for i in range(4)]
        sz = sb("sz", [128, 8, 512], BF)
        sz_b = [P.buf("sz%d" % i) for i in range(8)]
        uT2 = [sb("uT%d" % i, [128, 8, 512], BF) for i in range(2)]
        uT2_b = [P.buf("uT%d" % i) for i in range(2)]
        ogt = [sb("ogt%d" % i, [128, 8, 65], F32) for i in range(12)]
        ogt_b = [P.buf("ogt%d" % i) for i in range(12)]
        rden = [sb("rden%d" % i, [128, 8], F32) for i in range(4)]
        rden_b = [P.buf("rden%d" % i) for i in range(4)]
        oab = [sb("oab%d" % i, [128, 512], BF) for i in range(4)]
        oab_b = [P.buf("oab%d" % i) for i in range(4)]
        xo = [sb("xo%d" % i, [128, 1024], F32) for i in range(4)]
        xo_b = [P.buf("xo%d" % i) for i in range(4)]
        acnt_ = [0]
        xocnt = [0]

        def C_xa(ti):
            t0 = ti * 512
            for sub in range(4):
                xprep_a(x_t.ap()[t0 + sub * 128:t0 + (sub + 1) * 128, :], sub, c_nw0)

        def C_ol(ti):
            t0 = ti * 512
            for sub in range(4):
                tt = t0 + sub * 128
                for g in range(3):
                    oi_ = sub * 3 + g
                    P.add("sp", lambda e, g=g, oi_=oi_, tt=tt: e.dma_start(out=ogt[oi_][:].rearrange("p h c -> p (h c)"),
                                                                         in_=og.ap()[g, tt:tt + 128, :]),
                          writes=[ogt_b[oi_]], dma="ogt%d" % oi_)

        def C_xb(ti):
            for sub in range(4):
                xprep_b(sub, hTc2[ti % 2], hTc2_b[ti % 2], sub * 128, 7 if sub % 2 == 0 else 2)

        def C_s1(ti):
            t0 = ti * 512
            hTc, hTc_b, uT, uT_b = hTc2[ti % 2], hTc2_b[ti % 2], uT2[ti % 2], uT2_b[ti % 2]
            for zc in range(8):
                bank = zc % 2
                in_proj(wbz, wbz_b, 1536 + zc * 128, hTc, hTc_b, bank)
                P.add("act", lambda e, zc=zc, bank=bank: e.activation(out=sz[:, zc, :], in_=psum[bank][:], func=AF.Silu),
                      reads=[psb[bank]], writes=[sz_b[zc]])
            for cc in range(4):
                tb = cc % 2
                bcg, bhb = (2, 3) if cc % 2 == 0 else (6, 7)
                in_proj(wbz, wbz_b, 512 + cc * 128, hTc, hTc_b, bcg)
                in_proj(wbz, wbz_b, 1024 + cc * 128, hTc, hTc_b, bhb)
                in_proj(wbz, wbz_b, cc * 128, hTc, hTc_b, 4 + tb)
                P.add("act", lambda e, tb=tb, bcg=bcg: e.copy(out=tmpc[tb][:], in_=psum[bcg][:]), reads=[psb[bcg]], writes=[tmpc_b[tb]])
                P.add("dve", lambda e, tb=tb, cc=cc, bhb=bhb: e.tensor_tensor(out=gi[:, cc, 2:514], in0=tmpc[tb][:], in1=psum[bhb][:], op=ALU.mult),
                      reads=[tmpc_b[tb], psb[bhb]], writes=[gi_b[cc]])
                P.add("dve", lambda e, tb=tb, cc=cc: e.tensor_scalar(out=ycv[tb][:], in0=gi[:, cc, 0:512], scalar1=c_cw[:, cc, 0:1], scalar2=None,
                                                                   op0=ALU.mult), reads=[gi_b[cc], cc_b], writes=[ycv_b[tb]])
                for kk in (1, 2):
                    P.add("dve", lambda e, tb=tb, cc=cc, kk=kk: e.scalar_tensor_tensor(out=ycv[tb][:], in0=gi[:, cc, kk:kk + 512],
                                                                                    scalar=c_cw[:, cc, kk:kk + 1], in1=ycv[tb][:],
                                                                                    op0=ALU.mult, op1=ALU.add),
                          reads=[gi_b[cc], cc_b, ycv_b[tb]], writes=[ycv_b[tb]])
                P.add("dve", lambda e, tb=tb, cc=cc: e.tensor_tensor(out=yb[:, cc, :], in0=ycv[tb][:], in1=psum[4 + tb][:], op=ALU.mult),
                      reads=[ycv_b[tb], psb[4 + tb]], writes=[yb_b[cc]])
                P.add("pool", lambda e, cc=cc: e.tensor_copy(out=gi[:, cc, 0:2], in_=gi[:, cc, 512:514]), reads=[gi_b[cc]], writes=[gi_b[cc]])
                P.add("pool", lambda e, cc=cc: e.tensor_tensor(out=uT[:, 4 + cc, :], in0=yb[:, cc, :], in1=sz[:, 4 + cc, :], op=ALU.mult),
                      reads=[yb_b[cc], sz_b[4 + cc]], writes=[uT_b])

        def C_att_pre(ti):
            for sub in range(4):
                ai = sub
                a0, a1, a2 = sub * 3, sub * 3 + 1, sub * 3 + 2
                P.add("pool", lambda e, a0=a0, a1=a1: e.tensor_tensor(out=ogt[a0][:], in0=ogt[a0][:], in1=ogt[a1][:], op=ALU.add),
                      reads=[ogt_b[a0], ogt_b[a1]], writes=[ogt_b[a0]])
                P.add("pool", lambda e, a0=a0, a2=a2: e.tensor_tensor(out=ogt[a0][:], in0=ogt[a0][:], in1=ogt[a2][:], op=ALU.add),
                      reads=[ogt_b[a0], ogt_b[a2]], writes=[ogt_b[a0]])
                P.add("dve", lambda e, a0=a0, ai=ai: e.reciprocal(out=rden[ai][:], in_=ogt[a0][:, :, 64]), reads=[ogt_b[a0]],
                      writes=[rden_b[ai]])
                P.add("dve", lambda e, a0=a0, ai=ai: e.tensor_tensor(out=oab[ai][:].rearrange("p (h c) -> p h c", h=8), in0=ogt[a0][:, :, 0:64],
                                                                   in1=bc3(rden[ai][:, :], 64), op=ALU.mult),
                      reads=[ogt_b[a0], rden_b[ai]], writes=[oab_b[ai]])

        def C_att(ti):
            uT, uT_b = uT2[ti % 2], uT2_b[ti % 2]
            for sub in range(4):
                ai = sub
                pbk = 6 if sub % 2 == 0 else 5
                pv = psum[pbk][:].bitcast(BF)
                for cc in range(4):
                    P.add("pe", lambda e, cc=cc, ai=ai, pv=pv: e.transpose(out=pv[:, cc * 128:(cc + 1) * 128],
                                                                         in_=oab[ai][:, cc * 128:(cc + 1) * 128], identity=ident[:]),
                          reads=[oab_b[ai], b_const], writes=[psb[pbk]])
                P.add("dve", lambda e, sub=sub, pv=pv: e.tensor_tensor(out=uT[:, 0:4, sub * 128:(sub + 1) * 128],
                                                                     in0=pv[:, 0:512].rearrange("p (c t) -> p c t", c=4),
                                                                     in1=sz[:, 0:4, sub * 128:(sub + 1) * 128], op=ALU.mult),
                      reads=[psb[pbk], sz_b[0], sz_b[1], sz_b[2], sz_b[3]], writes=[uT_b])

        def C_out(ti):
            out_proj(uT2[ti % 2], uT2_b[ti % 2], wo0, wo0_b, xo, xo_b, x_t, x1, ti * 512, "xoC", (0, 1), xocnt)

        C_xa(0)
        C_ol(0)
        C_xb(0)
        C_att_pre(0)
        C_s1(0)
        C_att(0)
        C_xa(1)
        C_ol(1)
        C_xb(1)
        for ti in range(16):
            if ti + 2 < 16:
                C_xa(ti + 2)
            if ti + 1 < 16:
                C_att_pre(ti + 1)
                C_s1(ti + 1)
                C_att(ti + 1)
            if ti + 2 < 16:
                C_ol(ti + 2)
                C_xb(ti + 2)
            C_out(ti)
        final_dmas += ["xoC0", "xoC1", "xoC2", "xoC3"]
        P.barrier()
        cur[0].close()
        cur[0] = glob_stack

    if "D" in phases:
        glob_stack = cur[0]
        cur[0] = ExitStack()
        P.add("sp", lambda e: e.dma_start(out=c_nw0[:], in_=bass.AP(nw1, 0, [[0, 128], [1, D]])), writes=[b_nw], dma="const")
        w1 = sb("w1", [128, 8, 2560], BF)
        w1_b = P.buf("w1")
        load_weight(w1, w1_b, w_in1, 0, 2560)
        wo1 = sb("wo1", [128, 8, 1024], BF)
        wo1_b = P.buf("wo1")
        load_weight(wo1, wo1_b, w_out1, 0, 1024)
        cD = P.buf("cD")
        c_pw32 = sb("c_pw32", [128, 4, 128], F32)
        c_ps = sb("c_ps", [128, 4], F32)
        c_dw = sb("c_dw", [128, 4, NCONV], F32)
        c_db = sb("c_db", [128, 4], F32)
        c_lw = sb("c_lw", [128, 4], F32)
        c_lb = sb("c_lb", [128, 4], F32)
        c_ci = sb("c_ci", [128, 4, 16], F32)
        for dst_, src_ in [(c_pw32, poolw), (c_ps, pscale), (c_dw, dconvw), (c_db, dconvb), (c_lw, lnw), (c_lb, lnb), (c_ci, cinv)]:
            P.add("sp", lambda e, d_=dst_, s_=src_: e.dma_start(out=d_[:], in_=s_.ap()), writes=[cD], dma="cD")
        pw = sb("pw", [128, 4, 128], BF)
        pw_b = P.buf("pw")
        P.add("dve", lambda e: e.tensor_copy(out=pw[:], in_=c_pw32[:]), reads=[cD], writes=[pw_b])
        diag = sb("diag", [128, 4, NCONV, 128], BF)
        diag_b = P.buf("diag")
        for cc in range(4):
            for k in range(NCONV):
                eng = "dve"
                P.add(eng, lambda e, cc=cc, k=k: e.tensor_scalar(out=diag[:, cc, k, :], in0=ident[:], scalar1=c_dw[:, cc, k:k + 1], scalar2=None,
                                                               op0=ALU.mult), reads=[cD, b_const], writes=[diag_b])
        hTd = sb("hTd", [128, 8, 512], BF)
        hTd_b = P.buf("hTd")
        szd = sb("szD", [128, 8, 512], BF)
        szd_b = [P.buf("szD%d" % i) for i in range(8)]
        uce = sb("uce", [128, 4, 528], F32)
        uce_b = [P.buf("uce%d" % i) for i in range(4)]
        P.add("pool", lambda e: e.memset(uce[:], 0.0), writes=uce_b)
        sA = sb("sA", [128, 528], F32)
        sB = sb("sB", [128, 528], F32)
        sAB_b = P.buf("sAB")
        pooled = [sb("pooled%d" % i, [128, 512], BF) for i in range(4)]
        pooled_b = [P.buf("pooled%d" % i) for i in range(4)]
        gle = sb("gle", [128, 4, 544], BF)
        gle_b = [P.buf("gle%d" % i) for i in range(4)]
        P.add("pool", lambda e: e.memset(gle[:], 0.0), writes=gle_b)
        sg = [sb("sg%d" % i, [128, 512], F32) for i in range(2)]
        sg_b = [P.buf("sg%d" % i) for i in range(2)]
        c32 = sb("c32", [128, 4, 512], F32)
        c32_b = [P.buf("c32_%d" % i) for i in range(4)]
        cbf = sb("cbf", [128, 4, 512], BF)
        cbf_b = [P.buf("cbf%d" % i) for i in range(4)]
        csq = sb("csq", [128, 4, 512], BF)
        csq_b = [P.buf("csq%d" % i) for i in range(4)]
        mean = sb("mean", [128, 512], F32)
        var = sb("var", [128, 512], F32)
        stat_b = P.buf("stat")
        an = [sb("an%d" % i, [128, 512], F32) for i in range(2)]
        an_b = [P.buf("an%d" % i) for i in range(2)]
        uTd = sb("uTD", [128, 8, 512], BF)
        uTd_b = P.buf("uTD")
        xodcnt = [0]
        xod = [sb("xoD%d" % i, [128, 1024], F32) for i in range(3)]
        xod_b = [P.buf("xoD%d" % i) for i in range(3)]
        def D_xa(ti):
            t0 = ti * 512
            for sub in range(4):
                xprep_a(x1.ap()[t0 + sub * 128:t0 + (sub + 1) * 128, :], sub, c_nw1)

        def D_xb(ti):
            for sub in range(4):
                xprep_b(sub, hTd, hTd_b, sub * 128, 7 if sub % 2 == 0 else 6)

        def D_z(ti):
            for zc in range(8):
                bank = zc % 2
                in_proj(w1, w1_b, 1536 + zc * 128, hTd, hTd_b, bank)
                P.add("act", lambda e, zc=zc, bank=bank: e.activation(out=szd[:, zc, :], in_=psum[bank][:], func=AF.Silu),
                      reads=[psb[bank]], writes=[szd_b[zc]])

        def D_uc(ti):
            for gi_ in range(4):
                p = POOLS[gi_]
                pb = gi_
                ub = 2 + gi_ % 2
                in_proj(w1, w1_b, gi_ * 128, hTd, hTd_b, ub)
                P.add("act", lambda e, gi_=gi_, ub=ub: e.copy(out=uce[:, gi_, 16:528], in_=psum[ub][:]), reads=[psb[ub]], writes=[uce_b[gi_]])
                E = uce[:, gi_, :]
                P.add("dve", lambda e, E=E: e.tensor_tensor(out=sA[:, 1:528], in0=E[:, 1:528], in1=E[:, 0:527], op=ALU.add),
                      reads=[uce_b[gi_]], writes=[sAB_b])
                res = sA
                if p >= 4:
                    P.add("dve", lambda e: e.tensor_tensor(out=sB[:, 3:528], in0=sA[:, 3:528], in1=sA[:, 1:526], op=ALU.add),
                          reads=[sAB_b], writes=[sAB_b])
                    res = sB
                if p >= 8:
                    P.add("dve", lambda e: e.tensor_tensor(out=sA[:, 7:528], in0=sB[:, 7:528], in1=sB[:, 3:524], op=ALU.add),
                          reads=[sAB_b], writes=[sAB_b])
                    res = sA
                if p >= 16:
                    P.add("dve", lambda e: e.tensor_tensor(out=sB[:, 15:528], in0=sA[:, 15:528], in1=sA[:, 7:520], op=ALU.add),
                          reads=[sAB_b], writes=[sAB_b])
                    res = sB
                P.add("dve", lambda e, res=res, E=E, pb=pb, p=p: e.scalar_tensor_tensor(out=pooled[pb][:], in0=res[:, 16:528], scalar=1.0 / p,
                                                                                    in1=E[:, 16:528], op0=ALU.mult, op1=ALU.subtract),
                      reads=[sAB_b, uce_b[gi_]], writes=[pooled_b[pb]])
                if ti == 0:
                    P.add("dve", lambda e, res=res, gi_=gi_: e.tensor_tensor(out=res[:, 0:16], in0=res[:, 16:32], in1=c_ci[:, gi_, :], op=ALU.mult),
                          reads=[sAB_b, cD], writes=[sAB_b])
                    P.add("dve", lambda e, res=res, E=E, pb=pb: e.tensor_tensor(out=pooled[pb][:, 0:16], in0=res[:, 0:16], in1=E[:, 16:32],
                                                                              op=ALU.subtract),
                          reads=[sAB_b, uce_b[gi_]], writes=[pooled_b[pb]])
                P.add("dve", lambda e, gi_=gi_: e.tensor_copy(out=uce[:, gi_, 0:16], in_=uce[:, gi_, 512:528]), reads=[uce_b[gi_]],
                      writes=[uce_b[gi_]])

        def D_pool(ti):
            for gi_ in range(4):
                pb = gi_
                pkb = 3 if gi_ % 2 == 0 else 2
                P.add("pe", lambda e, gi_=gi_, pb=pb, pkb=pkb: e.matmul(psum[pkb][:], lhsT=pw[:, gi_, :], rhs=pooled[pb][:], start=True, stop=True),
                      reads=[pw_b, pooled_b[pb]], writes=[psb[pkb]])
                P.add("dve", lambda e, gi_=gi_, pkb=pkb: e.scalar_tensor_tensor(out=uTd[:, gi_, :], in0=psum[pkb][:], scalar=c_ps[:, gi_:gi_ + 1],
                                                                              in1=szd[:, gi_, :], op0=ALU.mult, op1=ALU.mult),
                      reads=[psb[pkb], cD, szd_b[gi_]], writes=[uTd_b])

        def D_glu(ti):
            for cc in range(4):
                sgi = cc % 2
                ba, bg_ = (4, 5) if cc % 2 == 0 else (2, 3)
                in_proj(w1, w1_b, 512 + cc * 128, hTd, hTd_b, ba)
                in_proj(w1, w1_b, 1024 + cc * 128, hTd, hTd_b, bg_)
                P.add("act", lambda e, sgi=sgi, bg_=bg_: e.activation(out=sg[sgi][:], in_=psum[bg_][:], func=AF.Sigmoid), reads=[psb[bg_]],
                      writes=[sg_b[sgi]])
                P.add("dve", lambda e, sgi=sgi, cc=cc, ba=ba: e.tensor_tensor(out=gle[:, cc, 32:544], in0=psum[ba][:], in1=sg[sgi][:], op=ALU.mult),
                      reads=[psb[ba], sg_b[sgi]], writes=[gle_b[cc]])

        def D_conv(ti):
            for cc in range(4):
                cb = 6 if cc % 2 == 0 else 3
                for k in range(NCONV):
                    P.add("pe", lambda e, cc=cc, k=k, cb=cb: e.matmul(psum[cb][:], lhsT=diag[:, cc, k, :], rhs=gle[:, cc, 2 + k:2 + k + 512],
                                                                    start=(k == 0), stop=(k == NCONV - 1)),
                          reads=[diag_b, gle_b[cc]], writes=[psb[cb]])
                P.add("dve", lambda e, cc=cc: e.tensor_copy(out=gle[:, cc, 0:32], in_=gle[:, cc, 512:544]), reads=[gle_b[cc]],
                      writes=[gle_b[cc]])
                P.add("act", lambda e, cc=cc, cb=cb: e.activation(out=c32[:, cc, :], in_=psum[cb][:], func=AF.Identity, bias=c_db[:, cc:cc + 1]),
                      reads=[psb[cb], cD], writes=[c32_b[cc]])
                P.add("act", lambda e, cc=cc, cb=cb: e.activation(out=csq[:, cc, :], in_=psum[cb][:], func=AF.Square, bias=c_db[:, cc:cc + 1]),
                      reads=[psb[cb], cD], writes=[csq_b[cc]])
                P.add("act", lambda e, cc=cc, cb=cb: e.activation(out=cbf[:, cc, :], in_=psum[cb][:], func=AF.Identity, bias=c_db[:, cc:cc + 1]),
                      reads=[psb[cb], cD], writes=[cbf_b[cc]])

        def D_ln(ti):
            for cc in range(4):
                P.add("pe", lambda e, cc=cc: e.matmul(psum[0][:], lhsT=onesall[:], rhs=cbf[:, cc, :], start=(cc == 0), stop=(cc == 3)),
                      reads=[cbf_b[cc], b_const], writes=[psb[0]])
            for cc in range(4):
                P.add("pe", lambda e, cc=cc: e.matmul(psum[1][:], lhsT=onesall[:], rhs=csq[:, cc, :], start=(cc == 0), stop=(cc == 3)),
                      reads=[csq_b[cc], b_const], writes=[psb[1]])
            P.add("dve", lambda e: e.tensor_scalar(out=mean[:], in0=psum[0][:], scalar1=1.0 / 512, scalar2=None, op0=ALU.mult),
                  reads=[psb[0]], writes=[stat_b])
            P.add("dve", lambda e: e.tensor_tensor(out=var[:], in0=mean[:], in1=mean[:], op=ALU.mult), reads=[stat_b], writes=[stat_b])
            P.add("dve", lambda e: e.scalar_tensor_tensor(out=var[:], in0=psum[1][:], scalar=1.0 / 512, in1=var[:], op0=ALU.mult,
                                                        op1=ALU.subtract), reads=[psb[1], stat_b], writes=[stat_b])
            P.add("act", lambda e: e.activation(out=var[:], in_=var[:], func=AF.Ln, bias=c_eps[:, 0:1]), reads=[stat_b, b_const],
                  writes=[stat_b])
            P.add("act", lambda e: e.activation(out=var[:], in_=var[:], func=AF.Exp, scale=-0.5), reads=[stat_b], writes=[stat_b])
            anl = (an[0], an[1], sg[0], sg[1])
            anl_b = (an_b[0], an_b[1], sg_b[0], sg_b[1])
            for cc in range(4):
                P.add("dve", lambda e, cc=cc: e.tensor_tensor(out=anl[cc][:], in0=c32[:, cc, :], in1=mean[:], op=ALU.subtract),
                      reads=[c32_b[cc], stat_b], writes=[anl_b[cc]])
                P.add("dve", lambda e, cc=cc: e.tensor_tensor(out=anl[cc][:], in0=anl[cc][:], in1=var[:], op=ALU.mult),
                      reads=[anl_b[cc], stat_b], writes=[anl_b[cc]])
            for cc in range(4):
                P.add("act", lambda e, cc=cc: e.activation(out=anl[cc][:], in_=anl[cc][:], func=AF.Silu, scale=c_lw[:, cc:cc + 1],
                                                         bias=c_lb[:, cc:cc + 1]), reads=[anl_b[cc], cD], writes=[anl_b[cc]])
            for cc in range(4):
                P.add("dve", lambda e, cc=cc: e.tensor_tensor(out=uTd[:, 4 + cc, :], in0=anl[cc][:], in1=szd[:, 4 + cc, :], op=ALU.mult),
                      reads=[anl_b[cc], szd_b[4 + cc]], writes=[uTd_b])

        def D_out(ti):
            if dbg:
                P.add("sp", lambda e: e.dma_start(out=dbgD.ap()[ti], in_=uTd[:]), reads=[uTd_b], dma="dbgD")
            out_proj(uTd, uTd_b, wo1, wo1_b, xod, xod_b, x1, out_t, ti * 512, "xoD", (0, 1), xodcnt)

        NT = 16
        D_xa(0)
        D_xb(0)
        D_xa(1)
        D_uc(0)
        D_glu(0)
        D_z(0)
        D_xb(1)
        D_pool(0)
        D_conv(0)
        for ti in range(NT):
            if ti + 2 < NT:
                D_xa(ti + 2)
            D_ln(ti)
            if ti + 1 < NT:
                D_uc(ti + 1)
                D_glu(ti + 1)
            D_out(ti)
            if ti + 1 < NT:
                D_z(ti + 1)
                if ti + 2 < NT:
                    D_xb(ti + 2)
                D_pool(ti + 1)
                D_conv(ti + 1)
        final_dmas += ["xoD0", "xoD1", "xoD2"]
        P.barrier()
        cur[0].close()
        cur[0] = glob_stack


    P.emit(final_dmas)
    return nc


def make_inputs(inp, b):
    bf = ml_dtypes.bfloat16
    f32 = np.float32
    m = np.arange(128)
    mm = m % 64
    inv_freq = np.power(f32(500000.0), -(np.arange(8, dtype=f32) / f32(8))).astype(f32)
    invf = np.where(mm < 16, inv_freq[mm % 8], f32(0)).astype(f32).reshape(128, 1)
    ssign = np.where(mm < 8, -1.0, np.where(mm < 16, 1.0, 0.0)).astype(f32).reshape(128, 1)
    rperm = np.zeros((128, 128), f32)
    for o in range(128):
        if o % 64 < 8:
            rperm[o + 8, o] = 1.0
        elif o % 64 < 16:
            rperm[o - 8, o] = 1.0
    onesblk = np.zeros((128, 128), f32)
    onesblk[:64, :64] = 1.0
    onesblk[64:, 64:] = 1.0
    kk = np.arange(128)[:, None]
    qq = np.arange(128)[None, :]
    mprev = (kk >= qq).astype(f32)
    mcur = (kk <= qq).astype(f32)
    reg = np.stack([mprev, mcur], 1)
    first = np.stack([np.zeros_like(mprev), mcur], 1)
    mask = np.stack([np.concatenate([reg, reg], 1).reshape(128, 512), np.concatenate([first, first], 1).reshape(128, 512)], 1)
    cinv = np.zeros((128, 4, 16), f32)
    for gi, p in enumerate(POOLS):
        cinv[:, gi, :] = 1.0 / np.minimum(np.arange(16) + 1, p)
    d = {
        "x": np.ascontiguousarray(inp["x"][b]),
        "pos": np.ascontiguousarray(inp["positions"][b].reshape(1, S)),
        "w_in0": np.ascontiguousarray(inp["e_w_in"][0]),
        "nw0": np.ascontiguousarray(inp["e_norm_w"][0].reshape(1, D)),
        "qkw": np.ascontiguousarray(np.stack([np.tile(inp["e_q_norm_w"][0], 2), np.tile(inp["e_k_norm_w"][0], 2)], 1)),
        "invf": invf, "ssign": ssign,
        "ident": np.eye(128, dtype=f32).astype(bf), "onesblk": onesblk.astype(bf), "rperm": rperm.astype(bf),
        "mask": mask.astype(bf),
        "convw0": np.ascontiguousarray(inp["e_conv_w"][0].reshape(3, 4, 128).transpose(2, 1, 0)),
        "w_out0": np.ascontiguousarray(inp["e_w_out"][0]),
        "nw1": np.ascontiguousarray(inp["o_norm_w"][0].reshape(1, D)),
        "w_in1": np.ascontiguousarray(inp["o_w_in"][0]),
        "w_out1": np.ascontiguousarray(inp["o_w_out"][0]),
        "poolw": np.ascontiguousarray(inp["o_pool_w"][0].transpose(1, 0, 2)),
        "pscale": np.ascontiguousarray(inp["o_pool_scale"][0].reshape(4, 128).T),
        "dconvw": np.ascontiguousarray(inp["o_dconv_w"][0].reshape(NCONV, 4, 128).transpose(2, 1, 0)),
        "dconvb": np.ascontiguousarray(inp["o_dconv_b"][0].reshape(4, 128).T),
        "lnw": np.ascontiguousarray(inp["o_ln_w"][0].reshape(4, 128).T),
        "lnb": np.ascontiguousarray(inp["o_ln_b"][0].reshape(4, 128).T),
        "cinv": cinv,
        "onesall": np.ones((128, 128), f32).astype(bf),
    }
    return {k: np.ascontiguousarray(v.astype(np.float32) if v.dtype == np.float64 else v) for k, v in d.items()}


_NC = {}


def kernel(**inputs):
    inp = {k: np.asarray(v) for k, v in inputs.items()}
    if "full" not in _NC:
        _NC["full"] = build("ABCD")
    nc = _NC["full"]
    in_maps = [make_inputs(inp, b) for b in range(8)]
    res = run_bass_kernel_spmd(nc, in_maps, core_ids=list(range(8)))
    return np.stack([np.asarray(r["out"], dtype=np.float32).reshape(S, D) for r in res.results], 0)
```
